# Optimizing a Trainium2 kernel written in Bass

```python
import math
import jax, jax.numpy as jnp
from jax import lax
import numpy as np

D_MODEL = 1024
BATCH = 16
SEQ = 2048
DEPTH = 1

CHUNK = 64
RET_HEADS = 4
RET_KEY_DIM = 128
RET_VAL_DIM = 256
RET_QK = RET_HEADS * RET_KEY_DIM
RET_V = RET_HEADS * RET_VAL_DIM
ATT_HEADS = 8
ATT_HEAD_DIM = 64
ATT_W = ATT_HEADS * ATT_HEAD_DIM
BAND_CHUNKS = 8
BAND = (BAND_CHUNKS + 1) * CHUNK
MAX_REL = 256
N_REL = CHUNK + MAX_REL
D_FF = -(-8 * D_MODEL // (3 * 256)) * 256
ROPE_BASE = 10000.0
EPS = 1e-6
NEG_INF = -1e30
IN_SIZES = (RET_QK, RET_QK, RET_V, RET_V, ATT_W, ATT_W, ATT_W, 2 * D_MODEL)
IN_SPLITS = tuple(int(s) for s in np.cumsum(IN_SIZES)[:-1])
N_IN = int(sum(IN_SIZES))

kernel_name = "hybrid_retention_chunkattn_gated_block"


def rmsnorm(x, g):
    xf = x.astype(jnp.float32)
    y = xf * lax.rsqrt(jnp.mean(xf * xf, axis=-1, keepdims=True) + EPS)
    return (y * g.astype(jnp.float32)).astype(x.dtype)


def rotary(x, pos):
    d = x.shape[-1]
    freqs = ROPE_BASE ** (-jnp.arange(0, d, 2, dtype=jnp.float32) / d)
    ang = pos[:, None] * freqs[None, :]
    cos = jnp.cos(ang)[None, :, None, :].astype(x.dtype)
    sin = jnp.sin(ang)[None, :, None, :].astype(x.dtype)
    x1, x2 = x[..., : d // 2], x[..., d // 2:]
    return jnp.concatenate([x1 * cos - x2 * sin, x1 * sin + x2 * cos], axis=-1)


def retention(q, k, v):
    B, S, H, dk = q.shape
    dv = v.shape[-1]
    nc = S // CHUNK
    dt = q.dtype
    log_g = jnp.log(1.0 - 2.0 ** (-5.0 - jnp.arange(H, dtype=jnp.float32)))
    p = jnp.arange(CHUNK, dtype=jnp.float32)
    intra = jnp.exp(log_g[:, None, None] * jnp.abs(p[:, None] - p[None, :])).astype(dt)
    q_dec = jnp.exp(log_g[None, :] * (p[:, None] + 1.0)).astype(dt)[None, :, :, None]
    k_dec = jnp.exp(log_g[None, :] * (CHUNK - 1.0 - p[:, None])).astype(dt)[None, :, :, None]
    chunk_dec = jnp.exp(log_g * CHUNK).astype(dt)[None, :, None, None]
    k = k * jnp.asarray(RET_KEY_DIM ** -0.5, dt)

    def to_chunks(t):
        return t.reshape(B, nc, CHUNK, H, t.shape[-1]).transpose(1, 0, 2, 3, 4)

    def step(state, inp):
        qi, ki, vi = inp
        s = jnp.einsum('bnhd,bmhd->bhnm', qi, ki) * intra[None]
        o = jnp.einsum('bhnm,bmhe->bnhe', s, vi)
        o = o + jnp.einsum('bnhd,bhde->bnhe', qi * q_dec, state)
        new_state = state * chunk_dec + jnp.einsum('bmhd,bmhe->bhde', ki * k_dec, vi)
        return new_state, o

    state0 = jnp.zeros((B, H, dk, dv), dt)
    _, o = lax.scan(step, state0, (to_chunks(q), to_chunks(k), to_chunks(v)))
    return o.transpose(1, 0, 2, 3, 4).reshape(B, S, H, dv)


def head_groupnorm(o):
    of = o.astype(jnp.float32)
    mu = jnp.mean(of, axis=-1, keepdims=True)
    var = jnp.mean(jnp.square(of - mu), axis=-1, keepdims=True)
    return ((of - mu) * lax.rsqrt(var + EPS)).astype(o.dtype)


def chunk_band_attention(q, k, v, rel_bias):
    B, S, H, dh = q.shape
    nc = S // CHUNK
    pad = BAND_CHUNKS * CHUNK
    kp = jnp.pad(k, ((0, 0), (pad, 0), (0, 0), (0, 0)))
    vp = jnp.pad(v, ((0, 0), (pad, 0), (0, 0), (0, 0)))
    n = jnp.arange(CHUNK)
    j = jnp.arange(BAND)
    rel = (pad + n[:, None]) - j[None, :]
    idx = jnp.clip(rel, -(CHUNK - 1), MAX_REL) + (CHUNK - 1)
    bias = rel_bias.astype(jnp.float32)[:, idx]
    scale = ATT_HEAD_DIM ** -0.5
    qc = q.reshape(B, nc, CHUNK, H, dh).transpose(1, 0, 2, 3, 4)

    def one_chunk(args):
        i, qi = args
        ki = lax.dynamic_slice_in_dim(kp, i * CHUNK, BAND, axis=1)
        vi = lax.dynamic_slice_in_dim(vp, i * CHUNK, BAND, axis=1)
        s = jnp.einsum('bnhd,bmhd->bhnm', qi, ki).astype(jnp.float32) * scale + bias[None]
        valid = j >= (BAND_CHUNKS - i) * CHUNK
        s = jnp.where(valid[None, None, None, :], s, NEG_INF)
        pr = jax.nn.softmax(s, axis=-1).astype(vi.dtype)
        return jnp.einsum('bhnm,bmhd->bnhd', pr, vi)

    o = lax.map(one_chunk, (jnp.arange(nc), qc))
    return o.transpose(1, 0, 2, 3, 4).reshape(B, S, H * dh)


def setup_inputs(seed: int = 0) -> dict:
    key = jax.random.key(seed)
    ks = jax.random.split(key, 16)
    f32 = jnp.float32

    def w(k, shape, fan_in):
        return jax.random.normal(k, shape, f32) * (fan_in ** -0.5)

    return {
        "x": jax.random.normal(ks[0], (BATCH, SEQ, D_MODEL), f32),
        "norm_mix": 1.0 + 0.05 * jax.random.normal(ks[1], (DEPTH, D_MODEL), f32),
        "w_in": w(ks[2], (DEPTH, D_MODEL, N_IN), D_MODEL),
        "b_gate": 0.02 * jax.random.normal(ks[3], (DEPTH, 2 * D_MODEL), f32),
        "rel_bias": 0.1 * jax.random.normal(ks[4], (DEPTH, ATT_HEADS, N_REL), f32),
        "w_ret_out": w(ks[5], (DEPTH, RET_V, D_MODEL), RET_V),
        "w_att_out": w(ks[6], (DEPTH, ATT_W, D_MODEL), ATT_W),
        "w_out": w(ks[7], (DEPTH, D_MODEL, D_MODEL), D_MODEL),
        "norm_ffn": 1.0 + 0.05 * jax.random.normal(ks[8], (DEPTH, D_MODEL), f32),
        "w_ffn_gate": w(ks[9], (DEPTH, D_MODEL, D_FF), D_MODEL),
        "w_ffn_up": w(ks[10], (DEPTH, D_MODEL, D_FF), D_MODEL),
        "w_ffn_down": w(ks[11], (DEPTH, D_FF, D_MODEL), D_FF),
        "norm_final": 1.0 + 0.05 * jax.random.normal(ks[12], (D_MODEL,), f32),
    }


def reference(x, norm_mix, w_in, b_gate, rel_bias, w_ret_out, w_att_out, w_out,
              norm_ffn, w_ffn_gate, w_ffn_up, w_ffn_down, norm_final):
    B, S, _ = x.shape
    pos = jnp.arange(S, dtype=jnp.float32)
    h = x
    for l in range(DEPTH):
        xn = rmsnorm(h, norm_mix[l])
        proj = xn @ w_in[l]
        rq, rk, rv, rg, aq, ak, av, gl = jnp.split(proj, IN_SPLITS, axis=-1)
        rq = rotary(rq.reshape(B, S, RET_HEADS, RET_KEY_DIM), pos)
        rk = rotary(rk.reshape(B, S, RET_HEADS, RET_KEY_DIM), pos)
        rv = rv.reshape(B, S, RET_HEADS, RET_VAL_DIM)
        ro = head_groupnorm(retention(rq, rk, rv)).reshape(B, S, RET_V)
        y_ret = (jax.nn.silu(rg) * ro) @ w_ret_out[l]
        ao = chunk_band_attention(aq.reshape(B, S, ATT_HEADS, ATT_HEAD_DIM),
                                  ak.reshape(B, S, ATT_HEADS, ATT_HEAD_DIM),
                                  av.reshape(B, S, ATT_HEADS, ATT_HEAD_DIM),
                                  rel_bias[l])
        y_att = ao @ w_att_out[l]
        gates = jax.nn.sigmoid(gl + b_gate[l])
        g_ret, g_att = gates[..., :D_MODEL], gates[..., D_MODEL:]
        h = h + (g_ret * y_ret + g_att * y_att) @ w_out[l]
        hn = rmsnorm(h, norm_ffn[l])
        h = h + (jax.nn.silu(hn @ w_ffn_gate[l]) * (hn @ w_ffn_up[l])) @ w_ffn_down[l]
    return rmsnorm(h, norm_final)
```

```python
import numpy as np
import concourse.bass as bass
import concourse.mybir as mybir
from concourse.bass_utils import run_bass_kernel_spmd

F32 = mybir.dt.float32
BF16 = mybir.dt.bfloat16
AF = mybir.ActivationFunctionType
ALU = mybir.AluOpType

NCORES = 8
D = 1024
SEQ = 2048
NSEQ = 2
CH = 64
DFF = 2816
NF = DFF // 128
EPS = 1e-6
SAME_ENG_SYNC = True

ENGS = ["pe", "act", "dve", "pool", "sp"]


class Tok:
    __slots__ = ("name", "w", "r", "dsem", "dcnt")

    def __init__(self, name):
        self.name = name
        self.w = None
        self.r = {}
        self.dsem = None
        self.dcnt = 0


class Sched:
    def __init__(self, nc):
        self.nc = nc
        self.q = {e: [] for e in ENGS}
        self.cnt = {e: 0 for e in ENGS}
        self.sem = {e: nc.alloc_semaphore("tl_" + e) for e in ENGS}
        self.seen = {}
        self.nsem = 0

    def _collect(self, eng, reads, writes):
        need = {}

        def add(dep):
            if dep[0] == "eng":
                e, idx = dep[1], dep[2]
                if e == eng and (eng in ("pe", "sp") or not SAME_ENG_SYNC):
                    return
                key = ("eng", e)
                sem = self.sem[e]
                v = idx
            else:
                key, sem, v = dep[1], dep[2], dep[3]
            if v > need.get(key, (None, 0))[1]:
                need[key] = (sem, v)

        for t in reads:
            if t.w is not None:
                add(t.w)
        for t in writes:
            if t.w is not None:
                add(t.w)
            for d in t.r.values():
                add(d)
        waits = []
        for key, (sem, v) in need.items():
            if self.seen.get((eng, key), 0) >= v:
                continue
            self.seen[(eng, key)] = v
            waits.append((sem, v))
        return waits

    def op(self, eng, fn, reads=(), writes=()):
        waits = self._collect(eng, reads, writes)
        self.cnt[eng] += 1
        idx = self.cnt[eng]
        self.q[eng].append(("op", waits, fn))
        for t in writes:
            t.w = ("eng", eng, idx)
            t.r = {}
        for t in reads:
            t.r[("eng", eng)] = ("eng", eng, idx)

    def dma(self, eng, out, in_, tok, reads=(), writes=(), **kw):
        waits = self._collect(eng, reads, writes)
        if tok.dsem is None:
            tok.dsem = self.nc.alloc_semaphore("d_" + tok.name)
            self.nsem += 1
        tok.dcnt += 16
        key = ("dma", tok.name)
        dep = ("dma", key, tok.dsem, tok.dcnt)
        self.q[eng].append(("dma", waits, out, in_, tok.dsem, kw))
        for t in writes:
            t.w = dep
            t.r = {}
        for t in reads:
            t.r[key] = dep

    def wait_tok(self, eng, tok):
        if tok.dsem is not None:
            self.q[eng].append(("wait", [(tok.dsem, tok.dcnt)]))

    def emit(self):
        nc = self.nc
        with nc.Block() as block:
            for eng, attach in (
                ("pe", block.tensor),
                ("act", block.scalar),
                ("dve", block.vector),
                ("pool", block.gpsimd),
                ("sp", block.sync),
            ):
                items = self.q[eng]
                sem = self.sem[eng]

                def body(e, items=items, sem=sem):
                    for it in items:
                        if it[0] == "op":
                            for s, v in it[1]:
                                e.wait_ge(s, v)
                            ins = it[2](e)
                            ins.then_inc(sem, 1)
                        elif it[0] == "dma":
                            for s, v in it[1]:
                                e.wait_ge(s, v)
                            e.dma_start(out=it[2], in_=it[3], **it[5]).then_inc(it[4], 16)
                        else:
                            for s, v in it[1]:
                                e.wait_ge(s, v)

                attach(body)
        self.q = {e: [] for e in ENGS}


def bc(ap, pos, n):
    a = [list(x) for x in ap.ap]
    a.insert(1 + pos, [0, n])
    return bass.AP(ap.tensor, ap.offset, a)


def dram_bc_rows(t, ncols, nrows=128):
    return bass.AP(t, 0, [[0, nrows], [1, ncols]])


class PsumRing:
    def __init__(self, nc):
        self.t = nc.alloc_psum_tensor("psum_all", [128, 4096], F32)
        self.tok = [Tok("ps%d" % i) for i in range(8)]
        self.p = 0

    NR = 6

    def fixed(self, b, n):
        return self.t[:, b * 512:(b + n) * 512], self.tok[b:b + n]

    def get(self, n=1):
        if n == 2 and self.p % 2 == 1:
            self.p = (self.p + 1) % self.NR
        b = self.p
        self.p = (self.p + n) % self.NR
        ap = self.t[:, b * 512:(b + n) * 512]
        return ap, self.tok[b:b + n]


def build(dbg=False, phases=(1, 2, 3), T=NSEQ * SEQ, seq=SEQ, h_in=False):
    nc = bass.Bass("TRN2", target_bir_lowering=False)
    S = Sched(nc)
    ext_in = lambda name, shape, dt=F32: nc.dram_tensor(name, list(shape), dt, kind="ExternalInput")
    skind = "ExternalOutput" if dbg else "Internal"

    x_d = ext_in("x", [T, D])
    out_d = nc.dram_tensor("out", [T, D], F32, kind="ExternalOutput")
    w1_d = ext_in("w_in_a", [D, 4608])
    wgl_d = ext_in("w_gl", [D, 2048])
    wro_d = ext_in("w_ret_out", [D, D])
    wao_d = ext_in("w_att_out", [512, D])
    wo_d = ext_in("w_out", [D, D])
    wg_d = ext_in("w_gate", [D, DFF])
    wu_d = ext_in("w_up", [D, DFF])
    wd_d = ext_in("w_down", [DFF, D])
    nmix_d = ext_in("norm_mix", [D])
    nffn_d = ext_in("norm_ffn", [D])
    nfin_d = ext_in("norm_final", [D])
    bg_d = ext_in("b_gate_t", [128, 16])
    relb_d = ext_in("relb_t", [128, 8, 640])
    cos_d = ext_in("cos_t", [128, seq])
    sina_d = ext_in("sin_a", [128, seq])
    sinb_d = ext_in("sin_b", [128, seq])
    intra_d = ext_in("intra_t", [128, 512])
    qdec_d = ext_in("qdec_t", [128, 2, 64])
    kdec_d = ext_in("kdec_t", [128, 512])
    cdec_d = ext_in("cdec_t", [128, 2])
    ident_d = ext_in("ident", [128, 128])

    XN = nc.dram_tensor("XN", [8, 128, T], BF16, kind=skind)
    GR = nc.dram_tensor("GR", [8, 128, T], BF16, kind=skind)
    AO = nc.dram_tensor("AO", [4, 128, T], BF16, kind=skind)
    H = nc.dram_tensor("H", [T, D], F32, kind=("ExternalInput" if h_in else skind))

    PS = PsumRing(nc)

    ident = nc.alloc_sbuf_tensor("ident_sb", [128, 128], BF16)
    t_ident = Tok("ident")
    S.dma("pool", ident[:], ident_d.ap(), t_ident, writes=[t_ident])
    eps_t = nc.alloc_sbuf_tensor("eps", [128, 1], F32)
    t_eps = Tok("eps")
    S.op("dve", lambda e: e.memset(eps_t[:], EPS), writes=[t_eps])

    NB = T // 128
    t_XN = [Tok("XN%d" % i) for i in range(NB)]
    t_GR = [Tok("GR%d" % i) for i in range(NB)]
    t_AO = [Tok("AO%d" % i) for i in range(NB)]
    t_H = [Tok("H%d" % i) for i in range(NB)]
    t_out = [Tok("out%d" % i) for i in range(4)]
    ch_XN, ch_GR, ch_AO, ch_H = Tok("chXN"), Tok("chGR"), Tok("chAO"), Tok("chH")

    def finalize(ch, toks):
        for t in toks:
            if t.w is not None:
                t.w = ("dma", ("dma", ch.name), ch.dsem, ch.dcnt)

    def rmsnorm_T(src, t_src, ns, gbc, t_gbc, dstT, t_dst, tmp):
        ss, t_ss = tmp["ss"], tmp["t_ss"]
        for s in range(ns):
            S.op("act", lambda e, s=s: e.activation(out=tmp["junk"][:], in_=src[:, s, :], func=AF.Square,
                                                    accum_out=ss[:, s:s + 1]),
                 reads=[t_src], writes=[tmp["t_junk"], t_ss])
        S.op("act", lambda e: e.activation(out=tmp["rms"][:, 0:ns], in_=ss[:, 0:ns], func=AF.Sqrt,
                                           bias=eps_t[:, 0:1], scale=1.0 / D),
             reads=[t_ss, t_eps], writes=[tmp["t_rms"]])
        S.op("dve", lambda e: e.reciprocal(tmp["rstd"][:, 0:ns], tmp["rms"][:, 0:ns]),
             reads=[tmp["t_rms"]], writes=[tmp["t_rstd"]])
        for s in range(ns):
            xn, t_xn = tmp["xn"][s % 2], tmp["t_xn"][s % 2]
            S.op("dve", lambda e, s=s, xn=xn: e.scalar_tensor_tensor(
                out=xn[:], in0=src[:, s, :], scalar=tmp["rstd"][:, s:s + 1], in1=gbc[:],
                op0=ALU.mult, op1=ALU.mult),
                 reads=[t_src, tmp["t_rstd"], t_gbc], writes=[t_xn])
            pT, t_pT = PS.get(1)
            pTb = pT.bitcast(BF16)

            def tr(e, xn=xn, pTb=pTb):
                for k in range(8):
                    ins = e.transpose(pTb[:, k * 128:(k + 1) * 128], xn[:, k * 128:(k + 1) * 128], ident[:])
                return ins
            S.op("pe", tr, reads=[t_xn, t_ident], writes=t_pT)
            S.op("act", lambda e, s=s, pTb=pTb: e.activation(
                out=dstT[:, :, s * 128:(s + 1) * 128], in_=pTb.rearrange("p (k j) -> p k j", k=8), func=AF.Copy),
                 reads=t_pT, writes=[t_dst])

    def norm_tmp(pfx):
        d = {}
        d["ss"] = nc.alloc_sbuf_tensor(pfx + "ss", [128, 4], F32)
        d["rms"] = nc.alloc_sbuf_tensor(pfx + "rms", [128, 4], F32)
        d["rstd"] = nc.alloc_sbuf_tensor(pfx + "rstd", [128, 4], F32)
        d["junk"] = nc.alloc_sbuf_tensor(pfx + "junk", [128, D], BF16)
        d["xn"] = [nc.alloc_sbuf_tensor(pfx + "xn%d" % i, [128, D], BF16) for i in range(2)]
        for k in ("ss", "rms", "rstd", "junk"):
            d["t_" + k] = Tok(pfx + k)
        d["t_xn"] = [Tok(pfx + "xn%d" % i) for i in range(2)]
        return d


    from contextlib import ExitStack

    def SB(es, name, shape, dt):
        return es.enter_context(nc.sbuf_tensor(name, list(shape), dt))

    def phase_mix(es):
        TT = 256
        w1 = SB(es, "w1", [128, 8, 4608], BF16)
        t_w1 = [Tok("w1_%d" % i) for i in range(9)]
        w1_v = w1_d.ap().rearrange("(k p) n -> p k n", p=128)
        for i in range(9):
            S.dma("pool", w1[:, :, i * 512:(i + 1) * 512], w1_v[:, :, i * 512:(i + 1) * 512], t_w1[i], writes=[t_w1[i]])
        gmix = SB(es, "gmix", [128, D], F32); t_gmix = Tok("gmix")
        S.dma("sp", gmix[:], dram_bc_rows(nmix_d, D), t_gmix, writes=[t_gmix])
        intra = SB(es, "intra", [128, 512], F32); t_intra = Tok("intra")
        S.dma("sp", intra[:], intra_d.ap(), t_intra, writes=[t_intra])
        kdec = SB(es, "kdec", [128, 512], F32); t_kdec = Tok("kdec")
        S.dma("sp", kdec[:], kdec_d.ap(), t_kdec, writes=[t_kdec])
        qdec = SB(es, "qdec", [128, 2, 64], F32); t_qdec = Tok("qdec")
        S.dma("sp", qdec[:], qdec_d.ap(), t_qdec, writes=[t_qdec])
        cdec = SB(es, "cdec", [128, 2], F32); t_cdec = Tok("cdec")
        S.dma("sp", cdec[:], cdec_d.ap(), t_cdec, writes=[t_cdec])
        expB = SB(es, "expB", [128, 8, 640], BF16); t_expB = Tok("expB")
        relb = SB(es, "relb", [128, 2, 640], F32); t_relb = Tok("relb")
        for hh in range(4):
            S.dma("sp", relb[:], relb_d.ap()[:, hh * 2:hh * 2 + 2, :], t_relb, writes=[t_relb])
            S.op("act", lambda e, hh=hh: e.activation(out=expB[:, hh * 2:hh * 2 + 2, :], in_=relb[:], func=AF.Exp),
                 reads=[t_relb], writes=[t_expB])
        S.op("dve", lambda e: e.memset(expB[64:128, :, 0:64], 0.0), writes=[t_expB])
        S.op("dve", lambda e: e.memset(expB[0:64, :, 576:640], 0.0), writes=[t_expB])

        x_sb = [SB(es, "x_sb%d" % i, [128, 1, D], F32) for i in range(2)]
        t_x = [Tok("x_sb%d" % i) for i in range(2)]
        tmp = norm_tmp_es(es, "a_")
        xnT = SB(es, "xnT", [128, 8, TT], BF16); t_xnT = Tok("xnT")
        cs = [SB(es, "cs%d" % i, [128, 2, TT], F32) for i in range(2)]
        t_cs = [Tok("cs%d" % i) for i in range(2)]
        rt = [SB(es, "rt%d" % i, [128, 2, TT], F32) for i in range(2)]
        t_rt = [Tok("rt%d" % i) for i in range(2)]
        rqT = SB(es, "rqT", [128, 4, TT], BF16); t_rqT = Tok("rqT")
        rq_m = SB(es, "rq_m", [128, 2, 4, TT], BF16); t_rqm = Tok("rq_m")
        rqd = SB(es, "rqd", [128, 4, TT], BF16); t_rqd = Tok("rqd")
        rkT = SB(es, "rkT", [128, 4, TT], BF16); t_rkT = Tok("rkT")
        ktok = [SB(es, "ktok%d" % i, [128, 2, 512], BF16) for i in range(2)]
        t_ktok = [Tok("ktok%d" % i) for i in range(2)]
        rv_sb = SB(es, "rv_sb", [128, 2, D], BF16); t_rv = [Tok("rv0"), Tok("rv1")]
        sg_sb = SB(es, "sg_sb", [128, 2, D], BF16); t_sg = [Tok("sg0"), Tok("sg1")]
        aq_m = SB(es, "aq_m", [128, 2, 4, TT], BF16); t_aqm = Tok("aq_m")
        akr = SB(es, "akr", [128, 4, 768], BF16); t_akr = [Tok("akr%d" % i) for i in range(3)]
        vr = SB(es, "vr", [128, 6, 8, 65], BF16); t_vr = [Tok("vr%d" % i) for i in range(6)]
        e_sb = [SB(es, "e_sb%d" % i, [128, 640], BF16) for i in range(2)]
        t_e = [Tok("e_sb%d" % i) for i in range(2)]
        p_sb = [SB(es, "p_sb%d" % i, [128, 640], BF16) for i in range(2)]
        t_p = [Tok("p_sb%d" % i) for i in range(2)]
        sT_sb = SB(es, "sT_sb", [128, 4, 128], BF16); t_sT = Tok("sT_sb")
        St = SB(es, "St", [128, 2, 2, 256], F32); t_S = Tok("St")
        Sbf = [SB(es, "Sbf%d" % i, [128, 2, 2, 2, 256], BF16) for i in range(2)]
        t_Sbf = [Tok("Sbf%d" % i) for i in range(2)]
        stats = SB(es, "stats", [128, 4, 6], F32); t_stats = Tok("stats")
        mv = SB(es, "mv", [128, 4, 2], F32); t_mv = Tok("mv")
        sd = SB(es, "sd", [128, 4], F32); t_sd = Tok("sd")
        rs = SB(es, "rs", [128, 4], F32); t_rs = Tok("rs")
        gnt = SB(es, "gnt", [128, D], F32); t_gnt = [Tok("gnt%d" % i) for i in range(4)]
        gr_sb = SB(es, "gr_sb", [128, D], BF16); t_gr = Tok("gr_sb")
        grT = SB(es, "grT", [128, 8, TT], BF16); t_grT = Tok("grT")
        rden = SB(es, "rden", [128, 8], F32); t_rden = Tok("rden")
        ao_sb = SB(es, "ao_sb", [128, 512], BF16); t_ao = Tok("ao_sb")
        aoT = SB(es, "aoT", [128, 4, TT], BF16); t_aoT = Tok("aoT")

        for buf, tk in ((rq_m, t_rqm), (aq_m, t_aqm), (ktok[0], t_ktok[0]), (ktok[1], t_ktok[1]),
                        (Sbf[0], t_Sbf[0]), (Sbf[1], t_Sbf[1])):
            S.op("pool", lambda e, buf=buf: e.memset(buf[:], 0.0), writes=[tk])
        S.op("pool", lambda e: e.memset(vr[:], 1.0), writes=t_vr)

        NTS = seq // TT
        nseq = T // seq

        def load_x(g):
            b = g % 2
            S.dma("sp", x_sb[b][:, 0, :], x_d.ap()[g * 128:(g + 1) * 128, :], t_x[b], writes=[t_x[b]])

        load_x(0)
        for q_ in range(nseq):
            S.op("dve", lambda e: e.memset(St[:], 0.0), writes=[t_S])
            S.op("pool", lambda e: e.memset(Sbf[0][:], 0.0), writes=[t_Sbf[0]])
            for tt in range(NTS):
                def _body(q_=q_, tt=tt):
                    tg = q_ * NTS + tt
                    tok0 = tg * TT
                    pos0 = tt * TT
                    cb = tg % 2
                    S.dma("sp", cs[cb][:, 0, :], cos_d.ap()[:, pos0:pos0 + TT], t_cs[cb], writes=[t_cs[cb]])
                    S.dma("sp", cs[cb][:, 1, :], sina_d.ap()[:, pos0:pos0 + TT], t_cs[cb], writes=[t_cs[cb]])
                    for s in range(2):
                        g = tg * 2 + s
                        if g + 1 < T // 128:
                            load_x(g + 1)
                        rmsnorm_T(x_sb[g % 2], t_x[g % 2], 1, gmix, t_gmix, xnT[:, :, s * 128:(s + 1) * 128], t_xnT, tmp)
                    S.dma("sp", XN.ap()[:, :, tok0:tok0 + TT].rearrange("k p t -> p k t"), xnT[:], ch_XN,
                          reads=[t_xnT], writes=t_XN[tg * 2:tg * 2 + 2])

                    def fm_proj(col0, nch):
                        ps_, t_ps_ = PS.get(1)

                        def mm(e, ps_=ps_):
                            for c in range(nch):
                                for k in range(8):
                                    ins = e.matmul(ps_[:, c * TT:(c + 1) * TT], w1[:, k, col0 + c * 128:col0 + (c + 1) * 128],
                                                   xnT[:, k, :], start=(k == 0), stop=(k == 7))
                            return ins
                        S.op("pe", mm, reads=[t_xnT, t_w1[col0 // 512]], writes=t_ps_)
                        return ps_.rearrange("p (c t) -> p c t", c=2), t_ps_

                    cosb = bc(cs[cb][:, 0, :], 0, 2)
                    sinb = bc(cs[cb][:, 1, :], 0, 2)
                    for which in range(2):
                        base = which * 512
                        PA, t_PA = fm_proj(base, 2)
                        PB, t_PB = fm_proj(base + 256, 2)
                        dst, t_dst = (rqT, t_rqT) if which == 0 else (rkT, t_rkT)
                        S.op("dve", lambda e, PA=PA: e.tensor_tensor(out=rt[0][:], in0=PA, in1=cosb, op=ALU.mult),
                             reads=t_PA + [t_cs[cb]], writes=[t_rt[0]])
                        S.op("dve", lambda e, PB=PB: e.tensor_tensor(out=rt[1][:], in0=PB, in1=sinb, op=ALU.mult),
                             reads=t_PB + [t_cs[cb]], writes=[t_rt[1]])
                        S.op("pool", lambda e, dst=dst: e.tensor_tensor(out=dst[:, 0:2, :], in0=rt[0][:], in1=rt[1][:], op=ALU.subtract),
                             reads=t_rt, writes=[t_dst])
                        S.op("dve", lambda e, PA=PA: e.tensor_tensor(out=rt[0][:], in0=PA, in1=sinb, op=ALU.mult),
                             reads=t_PA + [t_cs[cb]], writes=[t_rt[0]])
                        S.op("dve", lambda e, PB=PB: e.tensor_tensor(out=rt[1][:], in0=PB, in1=cosb, op=ALU.mult),
                             reads=t_PB + [t_cs[cb]], writes=[t_rt[1]])
                        S.op("pool", lambda e, dst=dst: e.tensor_tensor(out=dst[:, 2:4, :], in0=rt[0][:], in1=rt[1][:], op=ALU.add),
                             reads=t_rt, writes=[t_dst])
                    S.op("pool", lambda e: e.tensor_copy(out=rq_m[0:64, 0, :, :], in_=rqT[0:64, :, :]), reads=[t_rqT], writes=[t_rqm])
                    S.op("pool", lambda e: e.tensor_copy(out=rq_m[64:128, 1, :, :], in_=rqT[64:128, :, :]), reads=[t_rqT], writes=[t_rqm])
                    qdb = bc(qdec[:], 1, TT // 64)
                    for X in range(2):
                        S.op("dve", lambda e, X=X: e.tensor_tensor(
                            out=rqd[:, 2 * X:2 * X + 2, :].rearrange("p a (c n) -> p a c n", n=64),
                            in0=rqT[:, 2 * X:2 * X + 2, :].rearrange("p a (c n) -> p a c n", n=64), in1=qdb, op=ALU.mult),
                             reads=[t_rqT, t_qdec], writes=[t_rqd])
                    for i in range(2):
                        P_, t_P = fm_proj(1024 + i * 256, 2)
                        S.op("act", lambda e, P_=P_, i=i: e.activation(out=aq_m[0:64, 0, 2 * i:2 * i + 2, :], in_=P_[0:64], func=AF.Copy),
                             reads=t_P, writes=[t_aqm])
                        S.op("act", lambda e, P_=P_, i=i: e.activation(out=aq_m[64:128, 1, 2 * i:2 * i + 2, :], in_=P_[64:128], func=AF.Copy),
                             reads=t_P, writes=[t_aqm])
                    slot = (tt % 3)
                    for i in range(2):
                        P_, t_P = fm_proj(1536 + i * 256, 2)
                        S.op("act", lambda e, P_=P_, i=i, slot=slot: e.activation(
                            out=akr[:, 2 * i:2 * i + 2, slot * TT:(slot + 1) * TT], in_=P_, func=AF.Copy),
                             reads=t_P, writes=[t_akr[slot]])

                    for s in range(2):
                        def _body(s=s):
                            G = tt * 2 + s
                            tsl = slice(s * 128, (s + 1) * 128)

                            def tm_proj(col0):
                                ps_, t_ps_ = PS.get(1)

                                def mm(e, ps_=ps_):
                                    for k in range(8):
                                        ins = e.matmul(ps_, xnT[:, k, tsl], w1[:, k, col0:col0 + 512], start=(k == 0), stop=(k == 7))
                                    return ins
                                S.op("pe", mm, reads=[t_xnT, t_w1[col0 // 512]], writes=t_ps_)
                                return ps_, t_ps_
                            for i in range(2):
                                P_, t_P = tm_proj(2048 + i * 512)
                                S.op("act", lambda e, P_=P_, i=i: e.activation(out=rv_sb[:, s, i * 512:(i + 1) * 512], in_=P_, func=AF.Copy),
                                     reads=t_P, writes=[t_rv[s]])
                            for i in range(2):
                                P_, t_P = tm_proj(3072 + i * 512)
                                S.op("act", lambda e, P_=P_, i=i: e.activation(out=sg_sb[:, s, i * 512:(i + 1) * 512], in_=P_, func=AF.Silu),
                                     reads=t_P, writes=[t_sg[s]])
                            P_, t_P = tm_proj(4096)
                            vs = G % 6
                            S.op("act", lambda e, P_=P_, vs=vs: e.activation(out=vr[:, vs, :, 0:64], in_=P_.rearrange("p (h d) -> p h d", h=8),
                                                                             func=AF.Copy), reads=t_P, writes=[t_vr[vs]])
                            kb = G % 2
                            pT, t_pT = PS.get(1)
                            pTb = pT.bitcast(BF16)

                            def trk(e, pTb=pTb):
                                for c in range(4):
                                    ins = e.transpose(pTb[:, c * 128:(c + 1) * 128], rkT[:, c, tsl], ident[:])
                                return ins
                            S.op("pe", trk, reads=[t_rkT, t_ident], writes=t_pT)
                            for ci in range(2):
                                S.op("dve", lambda e, ci=ci, pTb=pTb, kb=kb: e.tensor_tensor(
                                    out=ktok[kb][64 * ci:64 * ci + 64, ci, :], in0=pTb[64 * ci:64 * ci + 64, 0:512],
                                    in1=kdec[64 * ci:64 * ci + 64, :], op=ALU.mult), reads=t_pT + [t_kdec], writes=[t_ktok[kb]])

                            def U_mm(ci):
                                U_, t_U = PS.get(2)
                                Uv = U_.rearrange("p (x r e) -> p x r e", x=2, r=2)

                                def mm(e, Uv=Uv, ci=ci):
                                    for h in range(4):
                                        pr, hs = h // 2, h % 2
                                        for X in range(2):
                                            c0 = (X * 2 + pr) * 128 + hs * 64
                                            ins = e.matmul(Uv[64 * hs:64 * hs + 64, X, pr, :], ktok[kb][:, ci, c0:c0 + 64],
                                                           rv_sb[:, s, h * 256:(h + 1) * 256], start=True, stop=True,
                                                           tile_position=(0, 64 * hs))
                                    return ins
                                S.op("pe", mm, reads=[t_ktok[kb], t_rv[s]], writes=t_U)
                                return Uv, t_U

                            def S_update(Uv, t_U, par):
                                for pr in range(2):
                                    S.op("dve", lambda e, pr=pr, Uv=Uv: e.scalar_tensor_tensor(
                                        out=St[:, :, pr, :], in0=St[:, :, pr, :], scalar=cdec[:, pr:pr + 1], in1=Uv[:, :, pr, :],
                                        op0=ALU.mult, op1=ALU.add), reads=t_U + [t_S, t_cdec], writes=[t_S])
                                S.op("pool", lambda e, par=par: e.tensor_copy(out=Sbf[par][0:64, 0], in_=St[0:64]), reads=[t_S], writes=[t_Sbf[par]])
                                S.op("pool", lambda e, par=par: e.tensor_copy(out=Sbf[par][64:128, 1], in_=St[64:128]), reads=[t_S], writes=[t_Sbf[par]])

                            U0, t_U0 = U_mm(0)
                            sps, t_sps = PS.get(1)

                            def smm(e, sps=sps):
                                for h in range(4):
                                    pr, hs = h // 2, h % 2
                                    for X in range(2):
                                        ins = e.matmul(sps[:, h * 128:(h + 1) * 128], rkT[:, X * 2 + pr, tsl], rq_m[:, hs, X * 2 + pr, tsl],
                                                       start=(X == 0), stop=(X == 1))
                                return ins
                            S.op("pe", smm, reads=[t_rkT, t_rqm], writes=t_sps)
                            S_update(U0, t_U0, 1)
                            S.op("dve", lambda e, sps=sps: e.tensor_tensor(out=sT_sb[:].rearrange("p h n -> p (h n)"), in0=sps, in1=intra[:], op=ALU.mult),
                                 reads=t_sps + [t_intra], writes=[t_sT])
                            U1, t_U1 = U_mm(1)
                            ops_, t_ops = PS.get(2)

                            def omm(e, ops_=ops_):
                                for h in range(4):
                                    pr, hs = h // 2, h % 2
                                    e.matmul(ops_[:, h * 256:(h + 1) * 256], sT_sb[:, h, :], rv_sb[:, s, h * 256:(h + 1) * 256], start=True, stop=False)
                                    for ci in range(2):
                                        for X in range(2):
                                            c0 = s * 128 + ci * 64
                                            ins = e.matmul(ops_[64 * ci:64 * ci + 64, h * 256:(h + 1) * 256], rqd[:, X * 2 + pr, c0:c0 + 64],
                                                           Sbf[ci][:, hs, X, pr, :], start=False, stop=(ci == 1 and X == 1),
                                                           tile_position=(0, 64 * ci))
                                return ins
                            S.op("pe", omm, reads=[t_sT, t_rv[s], t_rqd, t_Sbf[0], t_Sbf[1]], writes=t_ops)
                            S_update(U1, t_U1, 0)
                            for h in range(4):
                                S.op("dve", lambda e, h=h, ops_=ops_: e.bn_stats(out=stats[:, h, :], in_=ops_[:, h * 256:(h + 1) * 256]),
                                     reads=t_ops, writes=[t_stats])
                            for h in range(4):
                                S.op("dve", lambda e, h=h: e.bn_aggr(out=mv[:, h, :], in_=stats[:, h, :]), reads=[t_stats], writes=[t_mv])
                            S.op("act", lambda e: e.activation(out=sd[:], in_=mv[:, :, 1], func=AF.Sqrt, bias=eps_t[:, 0:1], scale=1.0),
                                 reads=[t_mv, t_eps], writes=[t_sd])
                            S.op("dve", lambda e: e.reciprocal(rs[:], sd[:]), reads=[t_sd], writes=[t_rs])
                            for h in range(4):
                                hsl = slice(h * 256, (h + 1) * 256)
                                S.op("dve", lambda e, h=h, hsl=hsl, ops_=ops_: e.scalar_tensor_tensor(
                                    out=gnt[:, hsl], in0=ops_[:, hsl], scalar=mv[:, h, 0:1], in1=sg_sb[:, s, hsl],
                                    op0=ALU.subtract, op1=ALU.mult), reads=t_ops + [t_mv, t_sg[s]], writes=[t_gnt[h]])
                                S.op("dve", lambda e, h=h, hsl=hsl: e.tensor_scalar(out=gr_sb[:, hsl], in0=gnt[:, hsl], scalar1=rs[:, h:h + 1], scalar2=None, op0=ALU.mult),
                                     reads=[t_gnt[h], t_rs], writes=[t_gr])
                            pT, t_pT = PS.get(1)
                            pTb = pT.bitcast(BF16)

                            def trg(e, pTb=pTb):
                                for k in range(8):
                                    ins = e.transpose(pTb[:, k * 128:(k + 1) * 128], gr_sb[:, k * 128:(k + 1) * 128], ident[:])
                                return ins
                            S.op("pe", trg, reads=[t_gr, t_ident], writes=t_pT)
                            S.op("act", lambda e, pTb=pTb: e.activation(out=grT[:, :, tsl], in_=pTb.rearrange("p (k j) -> p k j", k=8), func=AF.Copy),
                                 reads=t_pT, writes=[t_grT])

                            nb = min(4, G) + 1
                            ncol = nb * 128
                            oa, t_oa = PS.fixed(6, 2)
                            for h in range(8):
                                hc, hs = h // 2, h % 2
                                s2, t_s2 = PS.get(2)

                                def amm(e, s2=s2, hc=hc, hs=hs):
                                    for dl in range(nb):
                                        J = G - dl
                                        rc = ((J // 2) % 3) * TT + (J % 2) * 128
                                        ins = e.matmul(s2[:, dl * 128:(dl + 1) * 128], akr[:, hc, rc:rc + 128], aq_m[:, hs, hc, tsl],
                                                       start=True, stop=True)
                                    return ins
                                S.op("pe", amm, reads=t_akr + [t_aqm], writes=t_s2)
                                eb = h % 2
                                S.op("act", lambda e, s2=s2, eb=eb: e.activation(out=e_sb[eb][:, 0:ncol], in_=s2[:, 0:ncol], func=AF.Exp, scale=0.125),
                                     reads=t_s2, writes=[t_e[eb]])
                                S.op("dve", lambda e, eb=eb, h=h: e.tensor_tensor(out=p_sb[eb][:, 0:ncol], in0=e_sb[eb][:, 0:ncol],
                                                                                 in1=expB[:, h, 0:ncol], op=ALU.mult),
                                     reads=[t_e[eb], t_expB], writes=[t_p[eb]])
                                ob = (h // 4) * 512 + (h % 4) * 65

                                def avm(e, eb=eb, h=h, ob=ob, oa=oa):
                                    for dl in range(nb):
                                        J = G - dl
                                        ins = e.matmul(oa[:, ob:ob + 65], p_sb[eb][:, dl * 128:(dl + 1) * 128], vr[:, J % 6, h, :],
                                                       start=(dl == 0), stop=(dl == nb - 1))
                                    return ins
                                S.op("pe", avm, reads=[t_p[eb]] + t_vr, writes=t_oa)
                            for b2 in range(2):
                                oav = oa[:, b2 * 512:b2 * 512 + 260].rearrange("p (h d) -> p h d", d=65)
                                S.op("dve", lambda e, oav=oav, b2=b2: e.reciprocal(rden[:, b2 * 4:b2 * 4 + 4], oav[:, :, 64]),
                                     reads=t_oa, writes=[t_rden])
                                S.op("dve", lambda e, oav=oav, b2=b2: e.tensor_tensor(
                                    out=ao_sb[:, b2 * 256:(b2 + 1) * 256].rearrange("p (h d) -> p h d", d=64), in0=oav[:, :, 0:64],
                                    in1=bc(rden[:, b2 * 4:b2 * 4 + 4], 1, 64), op=ALU.mult), reads=t_oa + [t_rden], writes=[t_ao])
                            pT, t_pT = PS.get(1)
                            pTb = pT.bitcast(BF16)

                            def tra(e, pTb=pTb):
                                for k in range(4):
                                    ins = e.transpose(pTb[:, k * 128:(k + 1) * 128], ao_sb[:, k * 128:(k + 1) * 128], ident[:])
                                return ins
                            S.op("pe", tra, reads=[t_ao, t_ident], writes=t_pT)
                            S.op("act", lambda e, pTb=pTb: e.activation(out=aoT[:, :, tsl], in_=pTb[:, 0:512].rearrange("p (k j) -> p k j", k=4), func=AF.Copy),
                                 reads=t_pT, writes=[t_aoT])
                        _body()
                    S.dma("sp", GR.ap()[:, :, tok0:tok0 + TT].rearrange("k p t -> p k t"), grT[:], ch_GR,
                          reads=[t_grT], writes=t_GR[tg * 2:tg * 2 + 2])
                    S.dma("sp", AO.ap()[:, :, tok0:tok0 + TT].rearrange("k p t -> p k t"), aoT[:], ch_AO,
                          reads=[t_aoT], writes=t_AO[tg * 2:tg * 2 + 2])
                _body()

    def phase_merge(es):
        TT = 512
        NT = T // TT
        wgl = SB(es, "wgl", [128, 8, 2048], BF16)
        wro = SB(es, "wro", [128, 8, D], BF16)
        wao = SB(es, "wao", [128, 4, D], BF16)
        wo = SB(es, "wo", [128, 8, D], BF16)
        t_wgl = [Tok("wgl%d" % i) for i in range(4)]
        t_wro = [Tok("wro%d" % i) for i in range(2)]
        t_wao = [Tok("wao%d" % i) for i in range(2)]
        t_wo = [Tok("wo%d" % i) for i in range(2)]
        v = wgl_d.ap().rearrange("(k p) n -> p k n", p=128)
        for i in range(4):
            S.dma("pool", wgl[:, :, i * 512:(i + 1) * 512], v[:, :, i * 512:(i + 1) * 512], t_wgl[i], writes=[t_wgl[i]])
        for wsb, wdr, tk in ((wro, wro_d, t_wro), (wao, wao_d, t_wao), (wo, wo_d, t_wo)):
            v = wdr.ap().rearrange("(k p) n -> p k n", p=128)
            for i in range(2):
                S.dma("pool", wsb[:, :, i * 512:(i + 1) * 512], v[:, :, i * 512:(i + 1) * 512], tk[i], writes=[tk[i]])
        bg = SB(es, "bg", [128, 16], F32); t_bg = Tok("bg")
        S.dma("sp", bg[:], bg_d.ap(), t_bg, writes=[t_bg])
        x_sb = [SB(es, "xb%d" % i, [128, 4, D], F32) for i in range(2)]
        t_x = [Tok("xb%d" % i) for i in range(2)]
        xnT = [SB(es, "xnTb%d" % i, [128, 8, TT], BF16) for i in range(2)]
        t_xnT = [Tok("xnTb%d" % i) for i in range(2)]
        grT = [SB(es, "grTb%d" % i, [128, 8, TT], BF16) for i in range(2)]
        t_grT = [Tok("grTb%d" % i) for i in range(2)]
        aoT = [SB(es, "aoTb%d" % i, [128, 4, TT], BF16) for i in range(2)]
        t_aoT = [Tok("aoTb%d" % i) for i in range(2)]
        mT = SB(es, "mT", [128, 8, TT], BF16); t_mT = [Tok("mT%d" % i) for i in range(8)]
        gs = [SB(es, "gs%d" % i, [128, TT], F32) for i in range(4)]
        t_gs = [Tok("gs%d" % i) for i in range(4)]
        tt_ = [SB(es, "tt%d" % i, [128, TT], F32) for i in range(4)]
        t_tt = [Tok("tt%d" % i) for i in range(4)]

        def load(t):
            b = t % 2
            sl = slice(t * TT, (t + 1) * TT)
            S.dma("sp", x_sb[b][:], x_d.ap()[sl, :].rearrange("(s p) d -> p s d", p=128), t_x[b], writes=[t_x[b]])
            S.dma("sp", xnT[b][:], XN.ap()[:, :, sl].rearrange("k p t -> p k t"), t_xnT[b], reads=t_XN[t * 4:t * 4 + 4], writes=[t_xnT[b]])
            S.dma("sp", grT[b][:], GR.ap()[:, :, sl].rearrange("k p t -> p k t"), t_grT[b], reads=t_GR[t * 4:t * 4 + 4], writes=[t_grT[b]])
            S.dma("sp", aoT[b][:], AO.ap()[:, :, sl].rearrange("k p t -> p k t"), t_aoT[b], reads=t_AO[t * 4:t * 4 + 4], writes=[t_aoT[b]])

        load(0)
        for t in range(NT):
            def _body(t=t):
                b = t % 2
                if t + 1 < NT:
                    load(t + 1)
                for c in range(8):
                    csl = slice(c * 128, (c + 1) * 128)
                    yr, t_yr = PS.get(1)
                    ya, t_ya = PS.get(1)
                    glr, t_glr = PS.get(1)
                    gla, t_gla = PS.get(1)

                    def mm(e, yr=yr, ya=ya, glr=glr, gla=gla, csl=csl, c=c):
                        for k in range(8):
                            e.matmul(glr, wgl[:, k, csl], xnT[b][:, k, :], start=(k == 0), stop=(k == 7))
                        for k in range(8):
                            e.matmul(gla, wgl[:, k, 1024 + c * 128:1024 + (c + 1) * 128], xnT[b][:, k, :], start=(k == 0), stop=(k == 7))
                        for k in range(8):
                            e.matmul(yr, wro[:, k, csl], grT[b][:, k, :], start=(k == 0), stop=(k == 7))
                        for k in range(4):
                            ins = e.matmul(ya, wao[:, k, csl], aoT[b][:, k, :], start=(k == 0), stop=(k == 3))
                        return ins
                    S.op("pe", mm, reads=[t_xnT[b], t_grT[b], t_aoT[b]] + t_wgl + t_wro + t_wao, writes=t_yr + t_ya + t_glr + t_gla)
                    i0 = (c % 2) * 2
                    S.op("act", lambda e, glr=glr, i0=i0, c=c: e.activation(out=gs[i0][:], in_=glr, func=AF.Sigmoid, bias=bg[:, c:c + 1], scale=1.0),
                         reads=t_glr + [t_bg], writes=[t_gs[i0]])
                    S.op("act", lambda e, gla=gla, i0=i0, c=c: e.activation(out=gs[i0 + 1][:], in_=gla, func=AF.Sigmoid, bias=bg[:, 8 + c:9 + c], scale=1.0),
                         reads=t_gla + [t_bg], writes=[t_gs[i0 + 1]])
                    S.op("dve", lambda e, yr=yr, i0=i0: e.tensor_tensor(out=tt_[i0][:], in0=yr, in1=gs[i0][:], op=ALU.mult),
                         reads=t_yr + [t_gs[i0]], writes=[t_tt[i0]])
                    S.op("dve", lambda e, ya=ya, i0=i0: e.tensor_tensor(out=tt_[i0 + 1][:], in0=ya, in1=gs[i0 + 1][:], op=ALU.mult),
                         reads=t_ya + [t_gs[i0 + 1]], writes=[t_tt[i0 + 1]])
                    S.op("pool", lambda e, i0=i0, c=c: e.tensor_tensor(out=mT[:, c, :], in0=tt_[i0][:], in1=tt_[i0 + 1][:], op=ALU.add),
                         reads=[t_tt[i0], t_tt[i0 + 1]], writes=[t_mT[c]])
                for s in range(4):
                    for hf in range(2):
                        ops_, t_ops = PS.get(1)

                        def mm2(e, s=s, hf=hf, ops_=ops_):
                            for k in range(8):
                                ins = e.matmul(ops_, mT[:, k, s * 128:(s + 1) * 128], wo[:, k, hf * 512:(hf + 1) * 512], start=(k == 0), stop=(k == 7))
                            return ins
                        S.op("pe", mm2, reads=t_mT + t_wo, writes=t_ops)
                        S.op("dve", lambda e, s=s, hf=hf, ops_=ops_, b=b: e.tensor_tensor(
                            out=x_sb[b][:, s, hf * 512:(hf + 1) * 512], in0=ops_, in1=x_sb[b][:, s, hf * 512:(hf + 1) * 512], op=ALU.add),
                             reads=t_ops + [t_x[b]], writes=[t_x[b]])
                S.dma("sp", H.ap()[t * TT:(t + 1) * TT, :].rearrange("(s p) d -> p s d", p=128), x_sb[b][:], ch_H,
                      reads=[t_x[b]], writes=t_H[t * 4:t * 4 + 4])
            _body()


    def phase_ffn(es):
        TT = 256
        NT = T // TT
        wg = SB(es, "wg", [128, 8, DFF], BF16)
        wu = SB(es, "wu", [128, 8, DFF], BF16)
        wd = SB(es, "wd", [128, NF, D], BF16)
        grp = [(0, 6), (6, 12), (12, 17), (17, 22)]
        t_wg = [Tok("wg%d" % i) for i in range(4)]
        t_wu = [Tok("wu%d" % i) for i in range(4)]
        t_wd = [Tok("wd%d" % i) for i in range(4)]
        f2g = {}
        for gi, (a, b) in enumerate(grp):
            for f in range(a, b):
                f2g[f] = gi
        gffn = SB(es, "gffn", [128, D], F32)
        gfin = SB(es, "gfin", [128, D], F32)
        t_gffn, t_gfin = Tok("gffn"), Tok("gfin")
        S.dma("sp", gffn[:], dram_bc_rows(nffn_d, D), t_gffn, writes=[t_gffn])
        S.dma("sp", gfin[:], dram_bc_rows(nfin_d, D), t_gfin, writes=[t_gfin])
        wg_v = wg_d.ap().rearrange("(k p) n -> p k n", p=128)
        wu_v = wu_d.ap().rearrange("(k p) n -> p k n", p=128)
        wd_v = wd_d.ap().rearrange("(f p) n -> p f n", p=128)
        for gi, (a, b) in enumerate(grp):
            S.dma("pool", wg[:, :, a * 128:b * 128], wg_v[:, :, a * 128:b * 128], t_wg[gi], writes=[t_wg[gi]])
            S.dma("pool", wu[:, :, a * 128:b * 128], wu_v[:, :, a * 128:b * 128], t_wu[gi], writes=[t_wu[gi]])
        for gi, (a, b) in enumerate(grp):
            S.dma("pool", wd[:, a:b, :], wd_v[:, a:b, :], t_wd[gi], writes=[t_wd[gi]])

        h_sb = [SB(es, "h_sb%d" % i, [128, 2, D], F32) for i in range(2)]
        t_h = [Tok("h_sb%d" % i) for i in range(2)]
        o_sb = [SB(es, "o_sb%d" % i, [128, 2, D], F32) for i in range(2)]
        t_o = [Tok("o_sb%d" % i) for i in range(2)]
        hnT = SB(es, "hnT", [128, 8, TT], BF16)
        t_hnT = Tok("hnT")
        actT = SB(es, "actT", [128, NF, TT], BF16)
        t_actT = [Tok("actT%d" % f) for f in range(NF)]
        sg = [SB(es, "sg%d" % i, [128, TT], F32) for i in range(2)]
        t_sg = [Tok("sg%d" % i) for i in range(2)]
        tmp = norm_tmp_es(es, "f_")
        tmp2 = norm_tmp_es(es, "g_")

        def load(t):
            b = t % 2
            src = H.ap()[t * TT:(t + 1) * TT, :].rearrange("(s p) d -> p s d", p=128)
            S.dma("sp", h_sb[b][:], src, t_h[b], reads=t_H[t * 2:t * 2 + 2], writes=[t_h[b]])

        load(0)
        for t in range(NT):
            b = t % 2
            if t + 1 < NT:
                load(t + 1)
            rmsnorm_T(h_sb[b], t_h[b], 2, gffn, t_gffn, hnT, t_hnT, tmp)
            for f in range(NF):
                gps, t_gps = PS.get(1)
                ups, t_ups = PS.get(1)
                gi = f2g[f]

                def mm(e, f=f, gps=gps, ups=ups):
                    for k in range(8):
                        e.matmul(gps[:, 0:TT], wg[:, k, f * 128:(f + 1) * 128], hnT[:, k, :], start=(k == 0), stop=(k == 7))
                    for k in range(8):
                        ins = e.matmul(ups[:, 0:TT], wu[:, k, f * 128:(f + 1) * 128], hnT[:, k, :], start=(k == 0), stop=(k == 7))
                    return ins
                S.op("pe", mm, reads=[t_hnT, t_wg[gi], t_wu[gi]], writes=t_gps + t_ups)
                sgb, t_sgb = sg[f % 2], t_sg[f % 2]
                S.op("act", lambda e, gps=gps, sgb=sgb: e.activation(out=sgb[:], in_=gps[:, 0:TT], func=AF.Silu),
                     reads=t_gps, writes=[t_sgb])
                S.op("dve", lambda e, f=f, ups=ups, sgb=sgb: e.tensor_tensor(
                    out=actT[:, f, :], in0=ups[:, 0:TT], in1=sgb[:], op=ALU.mult),
                     reads=t_ups + [t_sgb], writes=[t_actT[f]])
            for s in range(2):
                for hf in range(2):
                    ops_, t_ops = PS.get(1)

                    def mm2(e, s=s, hf=hf, ops_=ops_):
                        for f in range(NF):
                            ins = e.matmul(ops_, actT[:, f, s * 128:(s + 1) * 128], wd[:, f, hf * 512:(hf + 1) * 512],
                                           start=(f == 0), stop=(f == NF - 1))
                        return ins
                    S.op("pe", mm2, reads=t_actT + t_wd, writes=t_ops)
                    S.op("dve", lambda e, s=s, hf=hf, ops_=ops_, b=b: e.tensor_tensor(
                        out=h_sb[b][:, s, hf * 512:(hf + 1) * 512], in0=ops_, in1=h_sb[b][:, s, hf * 512:(hf + 1) * 512],
                        op=ALU.add), reads=t_ops + [t_h[b]], writes=[t_h[b]])
            ss, rms, rstd = tmp2["ss"], tmp2["rms"], tmp2["rstd"]
            for s in range(2):
                S.op("act", lambda e, s=s, b=b: e.activation(out=tmp2["junk"][:], in_=h_sb[b][:, s, :], func=AF.Square,
                                                           accum_out=ss[:, s:s + 1]),
                     reads=[t_h[b]], writes=[tmp2["t_junk"], tmp2["t_ss"]])
            S.op("act", lambda e: e.activation(out=rms[:, 0:2], in_=ss[:, 0:2], func=AF.Sqrt, bias=eps_t[:, 0:1],
                                               scale=1.0 / D), reads=[tmp2["t_ss"], t_eps], writes=[tmp2["t_rms"]])
            S.op("dve", lambda e: e.reciprocal(rstd[:, 0:2], rms[:, 0:2]), reads=[tmp2["t_rms"]], writes=[tmp2["t_rstd"]])
            for s in range(2):
                S.op("dve", lambda e, s=s, b=b: e.scalar_tensor_tensor(
                    out=o_sb[b][:, s, :], in0=h_sb[b][:, s, :], scalar=rstd[:, s:s + 1], in1=gfin[:],
                    op0=ALU.mult, op1=ALU.mult), reads=[t_h[b], tmp2["t_rstd"], t_gfin], writes=[t_o[b]])
            dst = out_d.ap()[t * TT:(t + 1) * TT, :].rearrange("(s p) d -> p s d", p=128)
            S.dma("sp", dst, o_sb[b][:], t_out[t % 4], reads=[t_o[b]], writes=[t_out[t % 4]])
        for tk in t_out:
            S.wait_tok("sp", tk)

    def norm_tmp_es(es, pfx):
        d = {}
        d["ss"] = SB(es, pfx + "ss", [128, 4], F32)
        d["rms"] = SB(es, pfx + "rms", [128, 4], F32)
        d["rstd"] = SB(es, pfx + "rstd", [128, 4], F32)
        d["junk"] = SB(es, pfx + "junk", [128, D], BF16)
        d["xn"] = [SB(es, pfx + "xn%d" % i, [128, D], BF16) for i in range(2)]
        for k in ("ss", "rms", "rstd", "junk"):
            d["t_" + k] = Tok(pfx + k)
        d["t_xn"] = [Tok(pfx + "xn%d" % i) for i in range(2)]
        return d

    if 1 in phases:
        with ExitStack() as es:
            phase_mix(es)
            finalize(ch_XN, t_XN)
            finalize(ch_GR, t_GR)
            finalize(ch_AO, t_AO)
            S.emit()
    if 2 in phases:
        with ExitStack() as es:
            phase_merge(es)
            finalize(ch_H, t_H)
            S.emit()
    if 3 in phases:
        with ExitStack() as es:
            phase_ffn(es)
            S.emit()
    return nc


def host_prep(norm_mix, w_in, b_gate, rel_bias, w_ret_out, w_att_out, w_out,
              norm_ffn, w_ffn_gate, w_ffn_up, w_ffn_down, norm_final, seq=SEQ):
    f32 = np.float32
    A = lambda a: np.ascontiguousarray(np.asarray(a, dtype=f32))
    w_in = np.asarray(w_in, dtype=f32)[0]
    j = np.arange(128)
    perm_rq = np.concatenate([(2 * (ch % 2) + j // 64) * 128 + (ch // 2) * 64 + j % 64 for ch in range(4)])
    cols_a = np.concatenate([perm_rq, 512 + perm_rq, 3072 + np.arange(512), 3584 + np.arange(512),
                             1024 + np.arange(1024), 2048 + np.arange(1024), 4096 + np.arange(512)])
    m = {}
    m["w_in_a"] = A(w_in[:, cols_a])
    m["w_gl"] = A(w_in[:, 4608:6656])
    m["w_ret_out"] = A(np.asarray(w_ret_out)[0])
    m["w_att_out"] = A(np.asarray(w_att_out)[0])
    m["w_out"] = A(np.asarray(w_out)[0])
    m["w_gate"] = A(np.asarray(w_ffn_gate)[0])
    m["w_up"] = A(np.asarray(w_ffn_up)[0])
    m["w_down"] = A(np.asarray(w_ffn_down)[0])
    m["norm_mix"] = A(np.asarray(norm_mix)[0])
    m["norm_ffn"] = A(np.asarray(norm_ffn)[0])
    m["norm_final"] = A(np.asarray(norm_final))
    m["b_gate_t"] = A(np.asarray(b_gate, dtype=f32)[0].reshape(16, 128).T)
    rb = np.asarray(rel_bias, dtype=f32)[0]
    jj = np.arange(128)[:, None]
    col = np.arange(640)[None, :]
    idx = np.clip(col - jj, -63, 256) + 63
    m["relb_t"] = A(np.transpose(rb[:, idx], (1, 0, 2)))
    p = np.arange(128)
    freqs = (10000.0 ** (-np.arange(0, 128, 2, dtype=f32) / f32(128))).astype(f32)
    pos = np.arange(seq, dtype=f32)
    ang = (pos[None, :] * freqs[p % 64][:, None]).astype(f32)
    m["cos_t"] = A(np.cos(ang))
    m["sin_a"] = A(np.sin(ang))
    m["sin_b"] = A(np.sin(ang))
    log_g = np.log(1.0 - 2.0 ** (-5.0 - np.arange(4, dtype=np.float64)))
    sc = 128.0 ** -0.5
    mm_ = np.arange(128)[:, None]
    nn = np.arange(128)[None, :]
    blk = (mm_ // 64 == nn // 64)
    intra = np.zeros((128, 4, 128), np.float64)
    for h in range(4):
        intra[:, h, :] = np.where(blk, np.exp(log_g[h] * np.abs(nn % 64 - mm_ % 64)) * sc, 0.0)
    m["intra_t"] = A(intra.reshape(128, 512))
    qd = np.zeros((128, 2, 64), np.float64)
    cd = np.zeros((128, 2), np.float64)
    for pr in range(2):
        hh = 2 * pr + p // 64
        qd[:, pr, :] = np.exp(log_g[hh][:, None] * (np.arange(64)[None, :] + 1.0))
        cd[:, pr] = np.exp(log_g[hh] * 64.0)
    m["qdec_t"] = A(qd)
    m["cdec_t"] = A(cd)
    kd = np.zeros((128, 4, 128), np.float64)
    for ch in range(4):
        hh = 2 * (ch % 2) + np.arange(128) // 64
        kd[:, ch, :] = np.exp(log_g[hh][None, :] * (63.0 - (np.arange(128) % 64)[:, None])) * sc
    m["kdec_t"] = A(kd.reshape(128, 512))
    m["ident"] = np.eye(128, dtype=f32)
    return m


_NC = {}


def kernel(x, norm_mix, w_in, b_gate, rel_bias, w_ret_out, w_att_out, w_out,
           norm_ffn, w_ffn_gate, w_ffn_up, w_ffn_down, norm_final):
    x = np.asarray(x, dtype=np.float32)
    B, S_, D_ = x.shape
    assert (B, S_, D_) == (NCORES * NSEQ, SEQ, D)
    shared = host_prep(norm_mix, w_in, b_gate, rel_bias, w_ret_out, w_att_out, w_out,
                       norm_ffn, w_ffn_gate, w_ffn_up, w_ffn_down, norm_final)
    if "nc" not in _NC:
        _NC["nc"] = build()
    nc = _NC["nc"]
    xs = x.reshape(NCORES, NSEQ * SEQ, D)
    in_maps = []
    for c in range(NCORES):
        d = dict(shared)
        d["x"] = np.ascontiguousarray(xs[c])
        in_maps.append(d)
    res = run_bass_kernel_spmd(nc, in_maps, core_ids=list(range(NCORES)))
    out = np.stack([np.asarray(r["out"], dtype=np.float32) for r in res.results], axis=0)
    return out.reshape(B, S_, D_)
```

```python
import numpy as np
import concourse.bass as bass
import concourse.mybir as mybir
from concourse.bass_utils import run_bass_kernel_spmd

F32 = mybir.dt.float32
BF16 = mybir.dt.bfloat16
AF = mybir.ActivationFunctionType
ALU = mybir.AluOpType

NCORES = 8
D = 1024
SEQ = 2048
NSEQ = 2
CH = 64
DFF = 2816
NF = DFF // 128
EPS = 1e-6
SAME_ENG_SYNC = True

ENGS = ["pe", "act", "dve", "pool", "sp"]


class Tok:
    __slots__ = ("name", "w", "r", "dsem", "dcnt")

    def __init__(self, name):
        self.name = name
        self.w = None
        self.r = {}
        self.dsem = None
        self.dcnt = 0


class Sched:
    def __init__(self, nc):
        self.nc = nc
        self.q = {e: [] for e in ENGS}
        self.cnt = {e: 0 for e in ENGS}
        self.sem = {e: nc.alloc_semaphore("tl_" + e) for e in ENGS}
        self.seen = {}
        self.nsem = 0

    def _collect(self, eng, reads, writes):
        need = {}

        def add(dep):
            if dep[0] == "eng":
                e, idx = dep[1], dep[2]
                if e == eng and (eng in ("pe", "sp") or not SAME_ENG_SYNC):
                    return
                key = ("eng", e)
                sem = self.sem[e]
                v = idx
            else:
                key, sem, v = dep[1], dep[2], dep[3]
            if v > need.get(key, (None, 0))[1]:
                need[key] = (sem, v)

        for t in reads:
            if t.w is not None:
                add(t.w)
        for t in writes:
            if t.w is not None:
                add(t.w)
            for d in t.r.values():
                add(d)
        waits = []
        for key, (sem, v) in need.items():
            if self.seen.get((eng, key), 0) >= v:
                continue
            self.seen[(eng, key)] = v
            waits.append((sem, v))
        return waits

    def op(self, eng, fn, reads=(), writes=()):
        waits = self._collect(eng, reads, writes)
        self.cnt[eng] += 1
        idx = self.cnt[eng]
        self.q[eng].append(("op", waits, fn))
        for t in writes:
            t.w = ("eng", eng, idx)
            t.r = {}
        for t in reads:
            t.r[("eng", eng)] = ("eng", eng, idx)

    def dma(self, eng, out, in_, tok, reads=(), writes=(), **kw):
        waits = self._collect(eng, reads, writes)
        if tok.dsem is None:
            tok.dsem = self.nc.alloc_semaphore("d_" + tok.name)
            self.nsem += 1
        tok.dcnt += 16
        key = ("dma", tok.name)
        dep = ("dma", key, tok.dsem, tok.dcnt)
        self.q[eng].append(("dma", waits, out, in_, tok.dsem, kw))
        for t in writes:
            t.w = dep
            t.r = {}
        for t in reads:
            t.r[key] = dep

    def wait_tok(self, eng, tok):
        if tok.dsem is not None:
            self.q[eng].append(("wait", [(tok.dsem, tok.dcnt)]))

    def emit(self):
        nc = self.nc
        with nc.Block() as block:
            for eng, attach in (
                ("pe", block.tensor),
                ("act", block.scalar),
                ("dve", block.vector),
                ("pool", block.gpsimd),
                ("sp", block.sync),
            ):
                items = self.q[eng]
                sem = self.sem[eng]

                def body(e, items=items, sem=sem):
                    for it in items:
                        if it[0] == "op":
                            waits = it[1]
                            for s, v in waits[1:]:
                                e.wait_ge(s, v)
                            px = _Proxy(e)
                            ins = it[2](px)
                            if waits:
                                px.first._wait_ge(waits[0][0], waits[0][1])
                            ins.then_inc(sem, 1)
                        elif it[0] == "dma":
                            for s, v in it[1]:
                                e.wait_ge(s, v)
                            e.dma_start(out=it[2], in_=it[3], **it[5]).then_inc(it[4], 16)
                        else:
                            for s, v in it[1]:
                                e.wait_ge(s, v)

                attach(body)
        self.q = {e: [] for e in ENGS}


class _Proxy:
    def __init__(self, e):
        self._e = e
        self.first = None

    def __getattr__(self, name):
        f = getattr(self._e, name)

        def w(*a, **k):
            r = f(*a, **k)
            if self.first is None:
                self.first = r
            return r
        return w


def bc(ap, pos, n):
    a = [list(x) for x in ap.ap]
    a.insert(1 + pos, [0, n])
    return bass.AP(ap.tensor, ap.offset, a)


def dram_bc_rows(t, ncols, nrows=128):
    return bass.AP(t, 0, [[0, nrows], [1, ncols]])


class PsumRing:
    def __init__(self, nc):
        self.t = nc.alloc_psum_tensor("psum_all", [128, 4096], F32)
        self.tok = [Tok("ps%d" % i) for i in range(8)]
        self.p = 0

    NR = 6

    def fixed(self, b, n):
        return self.t[:, b * 512:(b + n) * 512], self.tok[b:b + n]

    def get(self, n=1):
        if n == 3 and self.p % 3 != 0:
            self.p = (self.p + 3 - self.p % 3) % self.NR
        if n == 2 and self.p % 2 == 1:
            self.p = (self.p + 1) % self.NR
        b = self.p
        self.p = (self.p + n) % self.NR
        ap = self.t[:, b * 512:(b + n) * 512]
        return ap, self.tok[b:b + n]


def build(dbg=False, phases=(1, 2, 3), T=NSEQ * SEQ, seq=SEQ, h_in=False, ffn_tiles=None):
    nc = bass.Bass("TRN2", target_bir_lowering=False)
    S = Sched(nc)
    ext_in = lambda name, shape, dt=F32: nc.dram_tensor(name, list(shape), dt, kind="ExternalInput")
    skind = "ExternalOutput" if dbg else "Internal"

    x_d = ext_in("x", [T, D])
    out_d = nc.dram_tensor("out", [T, D], F32, kind="ExternalOutput")
    w1_d = ext_in("w_in_a", [D, 4608])
    wgl_d = ext_in("w_gl", [D, 2048])
    wro_d = ext_in("w_ret_out", [D, D])
    wao_d = ext_in("w_att_out", [512, D])
    wo_d = ext_in("w_out", [D, D])
    wg_d = ext_in("w_gate", [D, DFF])
    wu_d = ext_in("w_up", [D, DFF])
    wd_d = ext_in("w_down", [DFF, D])
    nmix_d = ext_in("norm_mix", [D])
    nffn_d = ext_in("norm_ffn", [D])
    nfin_d = ext_in("norm_final", [D])
    bg_d = ext_in("b_gate_t", [128, 16])
    relb_d = ext_in("relb_t", [128, 8, 640])
    cos_d = ext_in("cos_t", [128, seq])
    sina_d = ext_in("sin_a", [128, seq])
    sinb_d = ext_in("sin_b", [128, seq])
    intra_d = ext_in("intra_t", [128, 512])
    qdec_d = ext_in("qdec_t", [128, 2, 64])
    kdec_d = ext_in("kdec_t", [128, 512])
    cdec_d = ext_in("cdec_t", [128, 2])
    ident_d = ext_in("ident", [128, 128])

    XN = nc.dram_tensor("XN", [8, 128, T], BF16, kind=skind)
    GR = nc.dram_tensor("GR", [8, 128, T], BF16, kind=skind)
    AO = nc.dram_tensor("AO", [4, 128, T], BF16, kind=skind)
    H = nc.dram_tensor("H", [T, D], F32, kind=("ExternalInput" if h_in else skind))

    PS = PsumRing(nc)

    ident = nc.alloc_sbuf_tensor("ident_sb", [128, 128], BF16)
    t_ident = Tok("ident")
    S.dma("pool", ident[:], ident_d.ap(), t_ident, writes=[t_ident])
    eps_t = nc.alloc_sbuf_tensor("eps", [128, 1], F32)
    t_eps = Tok("eps")
    S.op("dve", lambda e: e.memset(eps_t[:], EPS), writes=[t_eps])

    NB = T // 128
    t_XN = [Tok("XN%d" % i) for i in range(NB)]
    t_GR = [Tok("GR%d" % i) for i in range(NB)]
    t_AO = [Tok("AO%d" % i) for i in range(NB)]
    t_H = [Tok("H%d" % i) for i in range(NB)]
    t_out = [Tok("out%d" % i) for i in range(4)]
    ch_XN, ch_GR, ch_AO, ch_H = Tok("chXN"), Tok("chGR"), Tok("chAO"), Tok("chH")

    def finalize(ch, toks):
        for t in toks:
            if t.w is not None:
                t.w = ("dma", ("dma", ch.name), ch.dsem, ch.dcnt)

    def rmsnorm_T(src, t_src, ns, gbc, t_gbc, dstT, t_dst, tmp):
        ss, t_ss = tmp["ss"], tmp["t_ss"]
        for s in range(ns):
            S.op("act", lambda e, s=s: e.activation(out=tmp["junk"][:], in_=src[:, s, :], func=AF.Square,
                                                    accum_out=ss[:, s:s + 1]),
                 reads=[t_src], writes=[tmp["t_junk"], t_ss])
        S.op("act", lambda e: e.activation(out=tmp["rms"][:, 0:ns], in_=ss[:, 0:ns], func=AF.Sqrt,
                                           bias=eps_t[:, 0:1], scale=1.0 / D),
             reads=[t_ss, t_eps], writes=[tmp["t_rms"]])
        S.op("dve", lambda e: e.reciprocal(tmp["rstd"][:, 0:ns], tmp["rms"][:, 0:ns]),
             reads=[tmp["t_rms"]], writes=[tmp["t_rstd"]])
        for s in range(ns):
            xn, t_xn = tmp["xn"][s % 2], tmp["t_xn"][s % 2]
            S.op("dve", lambda e, s=s, xn=xn: e.scalar_tensor_tensor(
                out=xn[:], in0=src[:, s, :], scalar=tmp["rstd"][:, s:s + 1], in1=gbc[:],
                op0=ALU.mult, op1=ALU.mult),
                 reads=[t_src, tmp["t_rstd"], t_gbc], writes=[t_xn])
            pT, t_pT = PS.get(1)
            pTb = pT.bitcast(BF16)

            def tr(e, xn=xn, pTb=pTb):
                for k in range(8):
                    ins = e.transpose(pTb[:, k * 128:(k + 1) * 128], xn[:, k * 128:(k + 1) * 128], ident[:])
                return ins
            S.op("pe", tr, reads=[t_xn, t_ident], writes=t_pT)
            S.op("act", lambda e, s=s, pTb=pTb: e.activation(
                out=dstT[:, :, s * 128:(s + 1) * 128], in_=pTb.rearrange("p (k j) -> p k j", k=8), func=AF.Copy),
                 reads=t_pT, writes=[t_dst])

    def norm_tmp(pfx):
        d = {}
        d["ss"] = nc.alloc_sbuf_tensor(pfx + "ss", [128, 4], F32)
        d["rms"] = nc.alloc_sbuf_tensor(pfx + "rms", [128, 4], F32)
        d["rstd"] = nc.alloc_sbuf_tensor(pfx + "rstd", [128, 4], F32)
        d["junk"] = nc.alloc_sbuf_tensor(pfx + "junk", [128, D], BF16)
        d["xn"] = [nc.alloc_sbuf_tensor(pfx + "xn%d" % i, [128, D], BF16) for i in range(2)]
        for k in ("ss", "rms", "rstd", "junk"):
            d["t_" + k] = Tok(pfx + k)
        d["t_xn"] = [Tok(pfx + "xn%d" % i) for i in range(2)]
        return d


    from contextlib import ExitStack

    def SB(es, name, shape, dt):
        return es.enter_context(nc.sbuf_tensor(name, list(shape), dt))

    def phase_mix(es):
        TT = 256
        w1 = SB(es, "w1", [128, 8, 4608], BF16)
        t_w1 = [Tok("w1_%d" % i) for i in range(9)]
        w1_v = w1_d.ap().rearrange("(k p) n -> p k n", p=128)
        for i in range(9):
            S.dma("pool", w1[:, :, i * 512:(i + 1) * 512], w1_v[:, :, i * 512:(i + 1) * 512], t_w1[i], writes=[t_w1[i]])
        gmix = SB(es, "gmix", [128, D], F32); t_gmix = Tok("gmix")
        S.dma("sp", gmix[:], dram_bc_rows(nmix_d, D), t_gmix, writes=[t_gmix])
        intra = SB(es, "intra", [128, 512], F32); t_intra = Tok("intra")
        S.dma("sp", intra[:], intra_d.ap(), t_intra, writes=[t_intra])
        kdec = SB(es, "kdec", [128, 512], F32); t_kdec = Tok("kdec")
        S.dma("sp", kdec[:], kdec_d.ap(), t_kdec, writes=[t_kdec])
        qdec = SB(es, "qdec", [128, 2, 64], F32); t_qdec = Tok("qdec")
        S.dma("sp", qdec[:], qdec_d.ap(), t_qdec, writes=[t_qdec])
        cdec = SB(es, "cdec", [128, 2], F32); t_cdec = Tok("cdec")
        S.dma("sp", cdec[:], cdec_d.ap(), t_cdec, writes=[t_cdec])
        expB = SB(es, "expB", [128, 8, 640], BF16); t_expB = Tok("expB")
        relb = SB(es, "relb", [128, 2, 640], F32); t_relb = Tok("relb")
        for hh in range(4):
            S.dma("sp", relb[:], relb_d.ap()[:, hh * 2:hh * 2 + 2, :], t_relb, writes=[t_relb])
            S.op("act", lambda e, hh=hh: e.activation(out=expB[:, hh * 2:hh * 2 + 2, :], in_=relb[:], func=AF.Exp),
                 reads=[t_relb], writes=[t_expB])
        S.op("dve", lambda e: e.memset(expB[64:128, :, 0:64], 0.0), writes=[t_expB])
        S.op("dve", lambda e: e.memset(expB[0:64, :, 576:640], 0.0), writes=[t_expB])

        x_sb = [SB(es, "x_sb%d" % i, [128, 1, D], F32) for i in range(2)]
        t_x = [Tok("x_sb%d" % i) for i in range(2)]
        tmp = norm_tmp_es(es, "a_")
        xnT = SB(es, "xnT", [128, 8, TT], BF16); t_xnT = Tok("xnT")
        cs = [SB(es, "cs%d" % i, [128, 2, TT], F32) for i in range(2)]
        t_cs = [Tok("cs%d" % i) for i in range(2)]
        rt = [SB(es, "rt%d" % i, [128, 2, TT], F32) for i in range(2)]
        t_rt = [Tok("rt%d" % i) for i in range(2)]
        rqT = SB(es, "rqT", [128, 4, TT], BF16); t_rqT = Tok("rqT")
        rq_m = SB(es, "rq_m", [128, 2, 4, TT], BF16); t_rqm = Tok("rq_m")
        rqd = SB(es, "rqd", [128, 4, TT], BF16); t_rqd = Tok("rqd")
        rkT = SB(es, "rkT", [128, 4, TT], BF16); t_rkT = Tok("rkT")
        ktok = [SB(es, "ktok%d" % i, [128, 2, 512], BF16) for i in range(2)]
        t_ktok = [Tok("ktok%d" % i) for i in range(2)]
        rv_sb = SB(es, "rv_sb", [128, 2, D], BF16); t_rv = [Tok("rv0"), Tok("rv1")]
        sg_sb = SB(es, "sg_sb", [128, 2, D], BF16); t_sg = [Tok("sg0"), Tok("sg1")]
        aq_m = SB(es, "aq_m", [128, 2, 4, TT], BF16); t_aqm = Tok("aq_m")
        akr = SB(es, "akr", [128, 4, 768], BF16); t_akr = [Tok("akr%d" % i) for i in range(3)]
        vr = SB(es, "vr", [128, 6, 8, 65], BF16); t_vr = [Tok("vr%d" % i) for i in range(6)]
        e_sb = [SB(es, "e_sb%d" % i, [128, 1280], BF16) for i in range(2)]
        t_e = [Tok("e_sb%d" % i) for i in range(2)]
        p_sb = [SB(es, "p_sb%d" % i, [128, 1280], BF16) for i in range(2)]
        t_p = [Tok("p_sb%d" % i) for i in range(2)]
        sT_sb = SB(es, "sT_sb", [128, 4, 128], BF16); t_sT = Tok("sT_sb")
        St = SB(es, "St", [128, 2, 2, 256], F32); t_S = Tok("St")
        Sbf = [SB(es, "Sbf%d" % i, [128, 2, 2, 2, 256], BF16) for i in range(2)]
        t_Sbf = [Tok("Sbf%d" % i) for i in range(2)]
        stats = SB(es, "stats", [128, 4, 6], F32); t_stats = Tok("stats")
        mv = SB(es, "mv", [128, 4, 2], F32); t_mv = Tok("mv")
        sd = SB(es, "sd", [128, 4], F32); t_sd = Tok("sd")
        rs = SB(es, "rs", [128, 4], F32); t_rs = Tok("rs")
        gnt = SB(es, "gnt", [128, D], F32); t_gnt = [Tok("gnt%d" % i) for i in range(4)]
        gr_sb = SB(es, "gr_sb", [128, D], BF16); t_gr = Tok("gr_sb")
        grT = SB(es, "grT", [128, 8, TT], BF16); t_grT = Tok("grT")
        rden = SB(es, "rden", [128, 8], F32); t_rden = Tok("rden")
        ao_sb = SB(es, "ao_sb", [128, 512], BF16); t_ao = Tok("ao_sb")
        aoT = SB(es, "aoT", [128, 4, TT], BF16); t_aoT = Tok("aoT")

        for buf, tk in ((rq_m, t_rqm), (aq_m, t_aqm), (ktok[0], t_ktok[0]), (ktok[1], t_ktok[1]),
                        (Sbf[0], t_Sbf[0]), (Sbf[1], t_Sbf[1])):
            S.op("pool", lambda e, buf=buf: e.memset(buf[:], 0.0), writes=[tk])
        S.op("pool", lambda e: e.memset(vr[:], 1.0), writes=t_vr)

        NTS = seq // TT
        nseq = T // seq

        def load_x(g):
            b = g % 2
            S.dma("sp", x_sb[b][:, 0, :], x_d.ap()[g * 128:(g + 1) * 128, :], t_x[b], writes=[t_x[b]])

        load_x(0)
        for q_ in range(nseq):
            S.op("dve", lambda e: e.memset(St[:], 0.0), writes=[t_S])
            S.op("pool", lambda e: e.memset(Sbf[0][:], 0.0), writes=[t_Sbf[0]])
            for tt in range(NTS):
                def _body(q_=q_, tt=tt):
                    tg = q_ * NTS + tt
                    tok0 = tg * TT
                    pos0 = tt * TT
                    cb = tg % 2
                    S.dma("sp", cs[cb][:, 0, :], cos_d.ap()[:, pos0:pos0 + TT], t_cs[cb], writes=[t_cs[cb]])
                    S.dma("sp", cs[cb][:, 1, :], sina_d.ap()[:, pos0:pos0 + TT], t_cs[cb], writes=[t_cs[cb]])
                    for s in range(2):
                        g = tg * 2 + s
                        if g + 1 < T // 128:
                            load_x(g + 1)
                        rmsnorm_T(x_sb[g % 2], t_x[g % 2], 1, gmix, t_gmix, xnT[:, :, s * 128:(s + 1) * 128], t_xnT, tmp)
                    S.dma("sp", XN.ap()[:, :, tok0:tok0 + TT].rearrange("k p t -> p k t"), xnT[:], ch_XN,
                          reads=[t_xnT], writes=t_XN[tg * 2:tg * 2 + 2])

                    def fm_proj(col0, nch):
                        ps_, t_ps_ = PS.get(1)

                        def mm(e, ps_=ps_):
                            for c in range(nch):
                                for k in range(8):
                                    ins = e.matmul(ps_[:, c * TT:(c + 1) * TT], w1[:, k, col0 + c * 128:col0 + (c + 1) * 128],
                                                   xnT[:, k, :], start=(k == 0), stop=(k == 7))
                            return ins
                        S.op("pe", mm, reads=[t_xnT, t_w1[col0 // 512]], writes=t_ps_)
                        return ps_.rearrange("p (c t) -> p c t", c=2), t_ps_

                    cosb = bc(cs[cb][:, 0, :], 0, 2)
                    sinb = bc(cs[cb][:, 1, :], 0, 2)
                    for which in range(2):
                        base = which * 512
                        PA, t_PA = fm_proj(base, 2)
                        PB, t_PB = fm_proj(base + 256, 2)
                        dst, t_dst = (rqT, t_rqT) if which == 0 else (rkT, t_rkT)
                        S.op("dve", lambda e, PA=PA: e.tensor_tensor(out=rt[0][:], in0=PA, in1=cosb, op=ALU.mult),
                             reads=t_PA + [t_cs[cb]], writes=[t_rt[0]])
                        S.op("dve", lambda e, PB=PB: e.tensor_tensor(out=rt[1][:], in0=PB, in1=sinb, op=ALU.mult),
                             reads=t_PB + [t_cs[cb]], writes=[t_rt[1]])
                        S.op("pool", lambda e, dst=dst: e.tensor_tensor(out=dst[:, 0:2, :], in0=rt[0][:], in1=rt[1][:], op=ALU.subtract),
                             reads=t_rt, writes=[t_dst])
                        S.op("dve", lambda e, PA=PA: e.tensor_tensor(out=rt[0][:], in0=PA, in1=sinb, op=ALU.mult),
                             reads=t_PA + [t_cs[cb]], writes=[t_rt[0]])
                        S.op("dve", lambda e, PB=PB: e.tensor_tensor(out=rt[1][:], in0=PB, in1=cosb, op=ALU.mult),
                             reads=t_PB + [t_cs[cb]], writes=[t_rt[1]])
                        S.op("pool", lambda e, dst=dst: e.tensor_tensor(out=dst[:, 2:4, :], in0=rt[0][:], in1=rt[1][:], op=ALU.add),
                             reads=t_rt, writes=[t_dst])
                    S.op("pool", lambda e: e.tensor_copy(out=rq_m[0:64, 0, :, :], in_=rqT[0:64, :, :]), reads=[t_rqT], writes=[t_rqm])
                    S.op("pool", lambda e: e.tensor_copy(out=rq_m[64:128, 1, :, :], in_=rqT[64:128, :, :]), reads=[t_rqT], writes=[t_rqm])
                    qdb = bc(qdec[:], 1, TT // 64)
                    for X in range(2):
                        S.op("dve", lambda e, X=X: e.tensor_tensor(
                            out=rqd[:, 2 * X:2 * X + 2, :].rearrange("p a (c n) -> p a c n", n=64),
                            in0=rqT[:, 2 * X:2 * X + 2, :].rearrange("p a (c n) -> p a c n", n=64), in1=qdb, op=ALU.mult),
                             reads=[t_rqT, t_qdec], writes=[t_rqd])
                    for i in range(2):
                        P_, t_P = fm_proj(1024 + i * 256, 2)
                        S.op("act", lambda e, P_=P_, i=i: e.activation(out=aq_m[0:64, 0, 2 * i:2 * i + 2, :], in_=P_[0:64], func=AF.Copy),
                             reads=t_P, writes=[t_aqm])
                        S.op("act", lambda e, P_=P_, i=i: e.activation(out=aq_m[64:128, 1, 2 * i:2 * i + 2, :], in_=P_[64:128], func=AF.Copy),
                             reads=t_P, writes=[t_aqm])
                    slot = (tt % 3)
                    for i in range(2):
                        P_, t_P = fm_proj(1536 + i * 256, 2)
                        S.op("act", lambda e, P_=P_, i=i, slot=slot: e.activation(
                            out=akr[:, 2 * i:2 * i + 2, slot * TT:(slot + 1) * TT], in_=P_, func=AF.Copy),
                             reads=t_P, writes=[t_akr[slot]])

                    for s in range(2):
                        def _body(s=s):
                            G = tt * 2 + s
                            tsl = slice(s * 128, (s + 1) * 128)

                            def tm_proj(col0):
                                ps_, t_ps_ = PS.get(1)

                                def mm(e, ps_=ps_):
                                    for k in range(8):
                                        ins = e.matmul(ps_, xnT[:, k, tsl], w1[:, k, col0:col0 + 512], start=(k == 0), stop=(k == 7))
                                    return ins
                                S.op("pe", mm, reads=[t_xnT, t_w1[col0 // 512]], writes=t_ps_)
                                return ps_, t_ps_
                            for i in range(2):
                                P_, t_P = tm_proj(2048 + i * 512)
                                S.op("act", lambda e, P_=P_, i=i: e.activation(out=rv_sb[:, s, i * 512:(i + 1) * 512], in_=P_, func=AF.Copy),
                                     reads=t_P, writes=[t_rv[s]])
                            for i in range(2):
                                P_, t_P = tm_proj(3072 + i * 512)
                                S.op("act", lambda e, P_=P_, i=i: e.activation(out=sg_sb[:, s, i * 512:(i + 1) * 512], in_=P_, func=AF.Silu),
                                     reads=t_P, writes=[t_sg[s]])
                            P_, t_P = tm_proj(4096)
                            vs = G % 6
                            S.op("act", lambda e, P_=P_, vs=vs: e.activation(out=vr[:, vs, :, 0:64], in_=P_.rearrange("p (h d) -> p h d", h=8),
                                                                             func=AF.Copy), reads=t_P, writes=[t_vr[vs]])
                            kb = G % 2
                            pT, t_pT = PS.get(1)
                            pTb = pT.bitcast(BF16)

                            def trk(e, pTb=pTb):
                                for c in range(4):
                                    ins = e.transpose(pTb[:, c * 128:(c + 1) * 128], rkT[:, c, tsl], ident[:])
                                return ins
                            S.op("pe", trk, reads=[t_rkT, t_ident], writes=t_pT)
                            for ci in range(2):
                                S.op("dve", lambda e, ci=ci, pTb=pTb, kb=kb: e.tensor_tensor(
                                    out=ktok[kb][64 * ci:64 * ci + 64, ci, :], in0=pTb[64 * ci:64 * ci + 64, 0:512],
                                    in1=kdec[64 * ci:64 * ci + 64, :], op=ALU.mult), reads=t_pT + [t_kdec], writes=[t_ktok[kb]])

                            def U_mm(ci):
                                U_, t_U = PS.get(2)
                                Uv = U_.rearrange("p (x r e) -> p x r e", x=2, r=2)

                                def mm(e, Uv=Uv, ci=ci):
                                    for h in range(4):
                                        pr, hs = h // 2, h % 2
                                        for X in range(2):
                                            c0 = (X * 2 + pr) * 128 + hs * 64
                                            ins = e.matmul(Uv[64 * hs:64 * hs + 64, X, pr, :], ktok[kb][:, ci, c0:c0 + 64],
                                                           rv_sb[:, s, h * 256:(h + 1) * 256], start=True, stop=True,
                                                           tile_position=(0, 64 * hs))
                                    return ins
                                S.op("pe", mm, reads=[t_ktok[kb], t_rv[s]], writes=t_U)
                                return Uv, t_U

                            def S_update(Uv, t_U, par):
                                for pr in range(2):
                                    S.op("dve", lambda e, pr=pr, Uv=Uv: e.scalar_tensor_tensor(
                                        out=St[:, :, pr, :], in0=St[:, :, pr, :], scalar=cdec[:, pr:pr + 1], in1=Uv[:, :, pr, :],
                                        op0=ALU.mult, op1=ALU.add), reads=t_U + [t_S, t_cdec], writes=[t_S])
                                S.op("pool", lambda e, par=par: e.tensor_copy(out=Sbf[par][0:64, 0], in_=St[0:64]), reads=[t_S], writes=[t_Sbf[par]])
                                S.op("pool", lambda e, par=par: e.tensor_copy(out=Sbf[par][64:128, 1], in_=St[64:128]), reads=[t_S], writes=[t_Sbf[par]])

                            U0, t_U0 = U_mm(0)
                            sps, t_sps = PS.get(1)

                            def smm(e, sps=sps):
                                for h in range(4):
                                    pr, hs = h // 2, h % 2
                                    for X in range(2):
                                        ins = e.matmul(sps[:, h * 128:(h + 1) * 128], rkT[:, X * 2 + pr, tsl], rq_m[:, hs, X * 2 + pr, tsl],
                                                       start=(X == 0), stop=(X == 1))
                                return ins
                            S.op("pe", smm, reads=[t_rkT, t_rqm], writes=t_sps)
                            S_update(U0, t_U0, 1)
                            S.op("dve", lambda e, sps=sps: e.tensor_tensor(out=sT_sb[:].rearrange("p h n -> p (h n)"), in0=sps, in1=intra[:], op=ALU.mult),
                                 reads=t_sps + [t_intra], writes=[t_sT])
                            U1, t_U1 = U_mm(1)
                            ops_, t_ops = PS.get(2)

                            def omm(e, ops_=ops_):
                                for h in range(4):
                                    pr, hs = h // 2, h % 2
                                    e.matmul(ops_[:, h * 256:(h + 1) * 256], sT_sb[:, h, :], rv_sb[:, s, h * 256:(h + 1) * 256], start=True, stop=False)
                                    for ci in range(2):
                                        for X in range(2):
                                            c0 = s * 128 + ci * 64
                                            ins = e.matmul(ops_[64 * ci:64 * ci + 64, h * 256:(h + 1) * 256], rqd[:, X * 2 + pr, c0:c0 + 64],
                                                           Sbf[ci][:, hs, X, pr, :], start=False, stop=(ci == 1 and X == 1),
                                                           tile_position=(0, 64 * ci))
                                return ins
                            S.op("pe", omm, reads=[t_sT, t_rv[s], t_rqd, t_Sbf[0], t_Sbf[1]], writes=t_ops)
                            S_update(U1, t_U1, 0)
                            for h in range(4):
                                S.op("dve", lambda e, h=h, ops_=ops_: e.bn_stats(out=stats[:, h, :], in_=ops_[:, h * 256:(h + 1) * 256]),
                                     reads=t_ops, writes=[t_stats])
                            for h in range(4):
                                S.op("dve", lambda e, h=h: e.bn_aggr(out=mv[:, h, :], in_=stats[:, h, :]), reads=[t_stats], writes=[t_mv])
                            S.op("act", lambda e: e.activation(out=sd[:], in_=mv[:, :, 1], func=AF.Sqrt, bias=eps_t[:, 0:1], scale=1.0),
                                 reads=[t_mv, t_eps], writes=[t_sd])
                            S.op("dve", lambda e: e.reciprocal(rs[:], sd[:]), reads=[t_sd], writes=[t_rs])
                            for h in range(4):
                                hsl = slice(h * 256, (h + 1) * 256)
                                S.op("dve", lambda e, h=h, hsl=hsl, ops_=ops_: e.scalar_tensor_tensor(
                                    out=gnt[:, hsl], in0=ops_[:, hsl], scalar=mv[:, h, 0:1], in1=sg_sb[:, s, hsl],
                                    op0=ALU.subtract, op1=ALU.mult), reads=t_ops + [t_mv, t_sg[s]], writes=[t_gnt[h]])
                                S.op("dve", lambda e, h=h, hsl=hsl: e.tensor_scalar(out=gr_sb[:, hsl], in0=gnt[:, hsl], scalar1=rs[:, h:h + 1], scalar2=None, op0=ALU.mult),
                                     reads=[t_gnt[h], t_rs], writes=[t_gr])
                            pT, t_pT = PS.get(1)
                            pTb = pT.bitcast(BF16)

                            def trg(e, pTb=pTb):
                                for k in range(8):
                                    ins = e.transpose(pTb[:, k * 128:(k + 1) * 128], gr_sb[:, k * 128:(k + 1) * 128], ident[:])
                                return ins
                            S.op("pe", trg, reads=[t_gr, t_ident], writes=t_pT)
                            S.op("act", lambda e, pTb=pTb: e.activation(out=grT[:, :, tsl], in_=pTb.rearrange("p (k j) -> p k j", k=8), func=AF.Copy),
                                 reads=t_pT, writes=[t_grT])

                            nb = min(4, G) + 1
                            ncol = nb * 128
                            oa, t_oa = PS.fixed(6, 2)
                            for hc in range(4):
                                s3, t_s3 = PS.get(3)

                                def amm(e, s3=s3, hc=hc):
                                    for dl in range(nb):
                                        J = G - dl
                                        rc = ((J // 2) % 3) * TT + (J % 2) * 128
                                        ins = e.matmul(s3[:, dl * 256:(dl + 1) * 256], akr[:, hc, rc:rc + 128], aq_m[:, :, hc, tsl],
                                                       start=True, stop=True)
                                    return ins
                                S.op("pe", amm, reads=t_akr + [t_aqm], writes=t_s3)
                                eb = hc % 2
                                S.op("act", lambda e, s3=s3, eb=eb: e.activation(out=e_sb[eb][:, 0:2 * ncol], in_=s3[:, 0:2 * ncol], func=AF.Exp, scale=0.125),
                                     reads=t_s3, writes=[t_e[eb]])
                                S.op("dve", lambda e, eb=eb, hc=hc: e.tensor_tensor(
                                    out=p_sb[eb][:, 0:2 * ncol].rearrange("p (d h n) -> p d h n", h=2, n=128),
                                    in0=e_sb[eb][:, 0:2 * ncol].rearrange("p (d h n) -> p d h n", h=2, n=128),
                                    in1=expB[:, 2 * hc:2 * hc + 2, 0:ncol].rearrange("p h (d n) -> p d h n", n=128), op=ALU.mult),
                                     reads=[t_e[eb], t_expB], writes=[t_p[eb]])

                                def avm(e, eb=eb, hc=hc, oa=oa):
                                    for hs in range(2):
                                        h = 2 * hc + hs
                                        ob = (h // 4) * 512 + (h % 4) * 65
                                        for dl in range(nb):
                                            J = G - dl
                                            ins = e.matmul(oa[:, ob:ob + 65], p_sb[eb][:, dl * 256 + hs * 128:dl * 256 + hs * 128 + 128],
                                                           vr[:, J % 6, h, :], start=(dl == 0), stop=(dl == nb - 1))
                                    return ins
                                S.op("pe", avm, reads=[t_p[eb]] + t_vr, writes=t_oa)
                            for b2 in range(2):
                                oav = oa[:, b2 * 512:b2 * 512 + 260].rearrange("p (h d) -> p h d", d=65)
                                S.op("dve", lambda e, oav=oav, b2=b2: e.reciprocal(rden[:, b2 * 4:b2 * 4 + 4], oav[:, :, 64]),
                                     reads=t_oa, writes=[t_rden])
                                S.op("dve", lambda e, oav=oav, b2=b2: e.tensor_tensor(
                                    out=ao_sb[:, b2 * 256:(b2 + 1) * 256].rearrange("p (h d) -> p h d", d=64), in0=oav[:, :, 0:64],
                                    in1=bc(rden[:, b2 * 4:b2 * 4 + 4], 1, 64), op=ALU.mult), reads=t_oa + [t_rden], writes=[t_ao])
                            pT, t_pT = PS.get(1)
                            pTb = pT.bitcast(BF16)

                            def tra(e, pTb=pTb):
                                for k in range(4):
                                    ins = e.transpose(pTb[:, k * 128:(k + 1) * 128], ao_sb[:, k * 128:(k + 1) * 128], ident[:])
                                return ins
                            S.op("pe", tra, reads=[t_ao, t_ident], writes=t_pT)
                            S.op("act", lambda e, pTb=pTb: e.activation(out=aoT[:, :, tsl], in_=pTb[:, 0:512].rearrange("p (k j) -> p k j", k=4), func=AF.Copy),
                                 reads=t_pT, writes=[t_aoT])
                        _body()
                    S.dma("sp", GR.ap()[:, :, tok0:tok0 + TT].rearrange("k p t -> p k t"), grT[:], ch_GR,
                          reads=[t_grT], writes=t_GR[tg * 2:tg * 2 + 2])
                    S.dma("sp", AO.ap()[:, :, tok0:tok0 + TT].rearrange("k p t -> p k t"), aoT[:], ch_AO,
                          reads=[t_aoT], writes=t_AO[tg * 2:tg * 2 + 2])
                _body()

    def phase_merge(es):
        TT = 512
        NT = T // TT
        wgl = SB(es, "wgl", [128, 8, 2048], BF16)
        wro = SB(es, "wro", [128, 8, D], BF16)
        wao = SB(es, "wao", [128, 4, D], BF16)
        wo = SB(es, "wo", [128, 8, D], BF16)
        t_wgl = [Tok("wgl%d" % i) for i in range(4)]
        t_wro = [Tok("wro%d" % i) for i in range(2)]
        t_wao = [Tok("wao%d" % i) for i in range(2)]
        t_wo = [Tok("wo%d" % i) for i in range(2)]
        v = wgl_d.ap().rearrange("(k p) n -> p k n", p=128)
        for i in range(4):
            S.dma("pool", wgl[:, :, i * 512:(i + 1) * 512], v[:, :, i * 512:(i + 1) * 512], t_wgl[i], writes=[t_wgl[i]])
        for wsb, wdr, tk in ((wro, wro_d, t_wro), (wao, wao_d, t_wao), (wo, wo_d, t_wo)):
            v = wdr.ap().rearrange("(k p) n -> p k n", p=128)
            for i in range(2):
                S.dma("pool", wsb[:, :, i * 512:(i + 1) * 512], v[:, :, i * 512:(i + 1) * 512], tk[i], writes=[tk[i]])
        bg = SB(es, "bg", [128, 16], F32); t_bg = Tok("bg")
        S.dma("sp", bg[:], bg_d.ap(), t_bg, writes=[t_bg])
        x_sb = [SB(es, "xb%d" % i, [128, 4, D], F32) for i in range(2)]
        t_x = [Tok("xb%d" % i) for i in range(2)]
        xnT = [SB(es, "xnTb%d" % i, [128, 8, TT], BF16) for i in range(2)]
        t_xnT = [Tok("xnTb%d" % i) for i in range(2)]
        grT = [SB(es, "grTb%d" % i, [128, 8, TT], BF16) for i in range(2)]
        t_grT = [Tok("grTb%d" % i) for i in range(2)]
        aoT = [SB(es, "aoTb%d" % i, [128, 4, TT], BF16) for i in range(2)]
        t_aoT = [Tok("aoTb%d" % i) for i in range(2)]
        mT = SB(es, "mT", [128, 8, TT], BF16); t_mT = [Tok("mT%d" % i) for i in range(8)]
        gs = [SB(es, "gs%d" % i, [128, TT], F32) for i in range(4)]
        t_gs = [Tok("gs%d" % i) for i in range(4)]
        tt_ = [SB(es, "tt%d" % i, [128, TT], F32) for i in range(4)]
        t_tt = [Tok("tt%d" % i) for i in range(4)]

        def load(t):
            b = t % 2
            sl = slice(t * TT, (t + 1) * TT)
            S.dma("sp", x_sb[b][:], x_d.ap()[sl, :].rearrange("(s p) d -> p s d", p=128), t_x[b], writes=[t_x[b]])
            S.dma("sp", xnT[b][:], XN.ap()[:, :, sl].rearrange("k p t -> p k t"), t_xnT[b], reads=t_XN[t * 4:t * 4 + 4], writes=[t_xnT[b]])
            S.dma("sp", grT[b][:], GR.ap()[:, :, sl].rearrange("k p t -> p k t"), t_grT[b], reads=t_GR[t * 4:t * 4 + 4], writes=[t_grT[b]])
            S.dma("sp", aoT[b][:], AO.ap()[:, :, sl].rearrange("k p t -> p k t"), t_aoT[b], reads=t_AO[t * 4:t * 4 + 4], writes=[t_aoT[b]])

        load(0)
        for t in range(NT):
            def _body(t=t):
                b = t % 2
                if t + 1 < NT:
                    load(t + 1)
                for c in range(8):
                    csl = slice(c * 128, (c + 1) * 128)
                    yr, t_yr = PS.get(1)
                    ya, t_ya = PS.get(1)
                    glr, t_glr = PS.get(1)
                    gla, t_gla = PS.get(1)

                    def mm(e, yr=yr, ya=ya, glr=glr, gla=gla, csl=csl, c=c):
                        for k in range(8):
                            e.matmul(glr, wgl[:, k, csl], xnT[b][:, k, :], start=(k == 0), stop=(k == 7))
                        for k in range(8):
                            e.matmul(gla, wgl[:, k, 1024 + c * 128:1024 + (c + 1) * 128], xnT[b][:, k, :], start=(k == 0), stop=(k == 7))
                        for k in range(8):
                            e.matmul(yr, wro[:, k, csl], grT[b][:, k, :], start=(k == 0), stop=(k == 7))
                        for k in range(4):
                            ins = e.matmul(ya, wao[:, k, csl], aoT[b][:, k, :], start=(k == 0), stop=(k == 3))
                        return ins
                    S.op("pe", mm, reads=[t_xnT[b], t_grT[b], t_aoT[b]] + t_wgl + t_wro + t_wao, writes=t_yr + t_ya + t_glr + t_gla)
                    i0 = (c % 2) * 2
                    S.op("act", lambda e, glr=glr, i0=i0, c=c: e.activation(out=gs[i0][:], in_=glr, func=AF.Sigmoid, bias=bg[:, c:c + 1], scale=1.0),
                         reads=t_glr + [t_bg], writes=[t_gs[i0]])
                    S.op("act", lambda e, gla=gla, i0=i0, c=c: e.activation(out=gs[i0 + 1][:], in_=gla, func=AF.Sigmoid, bias=bg[:, 8 + c:9 + c], scale=1.0),
                         reads=t_gla + [t_bg], writes=[t_gs[i0 + 1]])
                    S.op("dve", lambda e, yr=yr, i0=i0: e.tensor_tensor(out=tt_[i0][:], in0=yr, in1=gs[i0][:], op=ALU.mult),
                         reads=t_yr + [t_gs[i0]], writes=[t_tt[i0]])
                    S.op("dve", lambda e, ya=ya, i0=i0: e.tensor_tensor(out=tt_[i0 + 1][:], in0=ya, in1=gs[i0 + 1][:], op=ALU.mult),
                         reads=t_ya + [t_gs[i0 + 1]], writes=[t_tt[i0 + 1]])
                    S.op("pool", lambda e, i0=i0, c=c: e.tensor_tensor(out=mT[:, c, :], in0=tt_[i0][:], in1=tt_[i0 + 1][:], op=ALU.add),
                         reads=[t_tt[i0], t_tt[i0 + 1]], writes=[t_mT[c]])
                for s in range(4):
                    for hf in range(2):
                        ops_, t_ops = PS.get(1)

                        def mm2(e, s=s, hf=hf, ops_=ops_):
                            for k in range(8):
                                ins = e.matmul(ops_, mT[:, k, s * 128:(s + 1) * 128], wo[:, k, hf * 512:(hf + 1) * 512], start=(k == 0), stop=(k == 7))
                            return ins
                        S.op("pe", mm2, reads=t_mT + t_wo, writes=t_ops)
                        S.op("dve", lambda e, s=s, hf=hf, ops_=ops_, b=b: e.tensor_tensor(
                            out=x_sb[b][:, s, hf * 512:(hf + 1) * 512], in0=ops_, in1=x_sb[b][:, s, hf * 512:(hf + 1) * 512], op=ALU.add),
                             reads=t_ops + [t_x[b]], writes=[t_x[b]])
                S.dma("sp", H.ap()[t * TT:(t + 1) * TT, :].rearrange("(s p) d -> p s d", p=128), x_sb[b][:], ch_H,
                      reads=[t_x[b]], writes=t_H[t * 4:t * 4 + 4])
            _body()


    def phase_ffn(es):
        wg = SB(es, "wg", [128, 8, DFF], BF16)
        wu = SB(es, "wu", [128, 8, DFF], BF16)
        wd = SB(es, "wd", [128, NF, D], BF16)
        grp = [(0, 6), (6, 12), (12, 17), (17, 22)]
        t_wg = [Tok("wg%d" % i) for i in range(4)]
        t_wu = [Tok("wu%d" % i) for i in range(4)]
        t_wd = [Tok("wd%d" % i) for i in range(4)]
        f2g = {}
        for gi, (a, b) in enumerate(grp):
            for f in range(a, b):
                f2g[f] = gi
        gffn = SB(es, "gffn", [128, D], F32)
        gfin = SB(es, "gfin", [128, D], F32)
        t_gffn, t_gfin = Tok("gffn"), Tok("gfin")
        S.dma("sp", gffn[:], dram_bc_rows(nffn_d, D), t_gffn, writes=[t_gffn])
        S.dma("sp", gfin[:], dram_bc_rows(nfin_d, D), t_gfin, writes=[t_gfin])
        wg_v = wg_d.ap().rearrange("(k p) n -> p k n", p=128)
        wu_v = wu_d.ap().rearrange("(k p) n -> p k n", p=128)
        wd_v = wd_d.ap().rearrange("(f p) n -> p f n", p=128)
        for gi, (a, b) in enumerate(grp):
            S.dma("pool", wg[:, :, a * 128:b * 128], wg_v[:, :, a * 128:b * 128], t_wg[gi], writes=[t_wg[gi]])
            S.dma("pool", wu[:, :, a * 128:b * 128], wu_v[:, :, a * 128:b * 128], t_wu[gi], writes=[t_wu[gi]])
        for gi, (a, b) in enumerate(grp):
            S.dma("pool", wd[:, a:b, :], wd_v[:, a:b, :], t_wd[gi], writes=[t_wd[gi]])

        TT = 512
        NT = T // TT
        if ffn_tiles is not None:
            NT = ffn_tiles
        NHB = 4
        h_sb = [SB(es, "h_sb%d" % i, [128, 1, D], F32) for i in range(NHB)]
        t_h = [Tok("h_sb%d" % i) for i in range(NHB)]
        hnT = SB(es, "hnT", [128, 8, TT], BF16)
        t_hnT = Tok("hnT")
        actT = SB(es, "actT", [128, NF, TT], BF16)
        t_actT = [Tok("actT%d" % f) for f in range(NF)]
        sg = [SB(es, "sg%d" % i, [128, TT], F32) for i in range(2)]
        t_sg = [Tok("sg%d" % i) for i in range(2)]
        tmp = norm_tmp_es(es, "f_")
        ctr = [0]

        def load_h(g):
            b = ctr[0] % NHB
            ctr[0] += 1
            S.dma("sp", h_sb[b][:, 0, :], H.ap()[g * 128:(g + 1) * 128, :], t_h[b], reads=[t_H[g]], writes=[t_h[b]])
            return b

        for t in range(NT):
            def _body(t=t):
                bufs = [load_h(t * 4 + s) for s in range(4)]
                for s in range(4):
                    rmsnorm_T(h_sb[bufs[s]], t_h[bufs[s]], 1, gffn, t_gffn, hnT[:, :, s * 128:(s + 1) * 128], t_hnT, tmp)
                bufs2 = [load_h(t * 4 + s) for s in range(4)]
                for f in range(NF):
                    gps, t_gps = PS.get(1)
                    ups, t_ups = PS.get(1)
                    gi = f2g[f]

                    def mm(e, f=f, gps=gps, ups=ups):
                        for k in range(8):
                            e.matmul(gps, wg[:, k, f * 128:(f + 1) * 128], hnT[:, k, :], start=(k == 0), stop=(k == 7))
                        for k in range(8):
                            ins = e.matmul(ups, wu[:, k, f * 128:(f + 1) * 128], hnT[:, k, :], start=(k == 0), stop=(k == 7))
                        return ins
                    S.op("pe", mm, reads=[t_hnT, t_wg[gi], t_wu[gi]], writes=t_gps + t_ups)
                    sgb, t_sgb = sg[f % 2], t_sg[f % 2]
                    S.op("act", lambda e, gps=gps, sgb=sgb: e.activation(out=sgb[:], in_=gps, func=AF.Silu),
                         reads=t_gps, writes=[t_sgb])
                    S.op("dve", lambda e, f=f, ups=ups, sgb=sgb: e.tensor_tensor(
                        out=actT[:, f, :], in0=ups, in1=sgb[:], op=ALU.mult),
                         reads=t_ups + [t_sgb], writes=[t_actT[f]])
                ss, rms, rstd = tmp["ss"], tmp["rms"], tmp["rstd"]
                for s in range(4):
                    hb, t_hb = h_sb[bufs2[s]], t_h[bufs2[s]]
                    for hf in range(2):
                        ops_, t_ops = PS.get(1)

                        def mm2(e, s=s, hf=hf, ops_=ops_):
                            for f in range(NF):
                                ins = e.matmul(ops_, actT[:, f, s * 128:(s + 1) * 128], wd[:, f, hf * 512:(hf + 1) * 512],
                                               start=(f == 0), stop=(f == NF - 1))
                            return ins
                        S.op("pe", mm2, reads=t_actT + t_wd, writes=t_ops)
                        S.op("dve", lambda e, hf=hf, ops_=ops_, hb=hb: e.tensor_tensor(
                            out=hb[:, 0, hf * 512:(hf + 1) * 512], in0=ops_, in1=hb[:, 0, hf * 512:(hf + 1) * 512],
                            op=ALU.add), reads=t_ops + [t_hb], writes=[t_hb])
                    S.op("act", lambda e, hb=hb: e.activation(out=tmp["junk"][:], in_=hb[:, 0, :], func=AF.Square,
                                                            accum_out=ss[:, 0:1]),
                         reads=[t_hb], writes=[tmp["t_junk"], tmp["t_ss"]])
                    S.op("act", lambda e: e.activation(out=rms[:, 0:1], in_=ss[:, 0:1], func=AF.Sqrt, bias=eps_t[:, 0:1],
                                                       scale=1.0 / D), reads=[tmp["t_ss"], t_eps], writes=[tmp["t_rms"]])
                    S.op("dve", lambda e: e.reciprocal(rstd[:, 0:1], rms[:, 0:1]), reads=[tmp["t_rms"]], writes=[tmp["t_rstd"]])
                    S.op("dve", lambda e, hb=hb: e.scalar_tensor_tensor(
                        out=hb[:, 0, :], in0=hb[:, 0, :], scalar=rstd[:, 0:1], in1=gfin[:],
                        op0=ALU.mult, op1=ALU.mult), reads=[t_hb, tmp["t_rstd"], t_gfin], writes=[t_hb])
                    g = t * 4 + s
                    S.dma("sp", out_d.ap()[g * 128:(g + 1) * 128, :], hb[:, 0, :], t_out[g % 4], reads=[t_hb], writes=[t_out[g % 4]])
            _body()
        for tk in t_out:
            S.wait_tok("sp", tk)

    def norm_tmp_es(es, pfx):
        d = {}
        d["ss"] = SB(es, pfx + "ss", [128, 4], F32)
        d["rms"] = SB(es, pfx + "rms", [128, 4], F32)
        d["rstd"] = SB(es, pfx + "rstd", [128, 4], F32)
        d["junk"] = SB(es, pfx + "junk", [128, D], BF16)
        d["xn"] = [SB(es, pfx + "xn%d" % i, [128, D], BF16) for i in range(2)]
        for k in ("ss", "rms", "rstd", "junk"):
            d["t_" + k] = Tok(pfx + k)
        d["t_xn"] = [Tok(pfx + "xn%d" % i) for i in range(2)]
        return d

    if 1 in phases:
        with ExitStack() as es:
            phase_mix(es)
            finalize(ch_XN, t_XN)
            finalize(ch_GR, t_GR)
            finalize(ch_AO, t_AO)
            S.emit()
    if 2 in phases:
        with ExitStack() as es:
            phase_merge(es)
            finalize(ch_H, t_H)
            S.emit()
    if 3 in phases:
        with ExitStack() as es:
            phase_ffn(es)
            S.emit()
    return nc


def host_prep(norm_mix, w_in, b_gate, rel_bias, w_ret_out, w_att_out, w_out,
              norm_ffn, w_ffn_gate, w_ffn_up, w_ffn_down, norm_final, seq=SEQ):
    f32 = np.float32
    A = lambda a: np.ascontiguousarray(np.asarray(a, dtype=f32))
    w_in = np.asarray(w_in, dtype=f32)[0]
    j = np.arange(128)
    perm_rq = np.concatenate([(2 * (ch % 2) + j // 64) * 128 + (ch // 2) * 64 + j % 64 for ch in range(4)])
    cols_a = np.concatenate([perm_rq, 512 + perm_rq, 3072 + np.arange(512), 3584 + np.arange(512),
                             1024 + np.arange(1024), 2048 + np.arange(1024), 4096 + np.arange(512)])
    m = {}
    m["w_in_a"] = A(w_in[:, cols_a])
    m["w_gl"] = A(w_in[:, 4608:6656])
    m["w_ret_out"] = A(np.asarray(w_ret_out)[0])
    m["w_att_out"] = A(np.asarray(w_att_out)[0])
    m["w_out"] = A(np.asarray(w_out)[0])
    m["w_gate"] = A(np.asarray(w_ffn_gate)[0])
    m["w_up"] = A(np.asarray(w_ffn_up)[0])
    m["w_down"] = A(np.asarray(w_ffn_down)[0])
    m["norm_mix"] = A(np.asarray(norm_mix)[0])
    m["norm_ffn"] = A(np.asarray(norm_ffn)[0])
    m["norm_final"] = A(np.asarray(norm_final))
    m["b_gate_t"] = A(np.asarray(b_gate, dtype=f32)[0].reshape(16, 128).T)
    rb = np.asarray(rel_bias, dtype=f32)[0]
    jj = np.arange(128)[:, None]
    col = np.arange(640)[None, :]
    idx = np.clip(col - jj, -63, 256) + 63
    m["relb_t"] = A(np.transpose(rb[:, idx], (1, 0, 2)))
    p = np.arange(128)
    freqs = (10000.0 ** (-np.arange(0, 128, 2, dtype=f32) / f32(128))).astype(f32)
    pos = np.arange(seq, dtype=f32)
    ang = (pos[None, :] * freqs[p % 64][:, None]).astype(f32)
    m["cos_t"] = A(np.cos(ang))
    m["sin_a"] = A(np.sin(ang))
    m["sin_b"] = A(np.sin(ang))
    log_g = np.log(1.0 - 2.0 ** (-5.0 - np.arange(4, dtype=np.float64)))
    sc = 128.0 ** -0.5
    mm_ = np.arange(128)[:, None]
    nn = np.arange(128)[None, :]
    blk = (mm_ // 64 == nn // 64)
    intra = np.zeros((128, 4, 128), np.float64)
    for h in range(4):
        intra[:, h, :] = np.where(blk, np.exp(log_g[h] * np.abs(nn % 64 - mm_ % 64)) * sc, 0.0)
    m["intra_t"] = A(intra.reshape(128, 512))
    qd = np.zeros((128, 2, 64), np.float64)
    cd = np.zeros((128, 2), np.float64)
    for pr in range(2):
        hh = 2 * pr + p // 64
        qd[:, pr, :] = np.exp(log_g[hh][:, None] * (np.arange(64)[None, :] + 1.0))
        cd[:, pr] = np.exp(log_g[hh] * 64.0)
    m["qdec_t"] = A(qd)
    m["cdec_t"] = A(cd)
    kd = np.zeros((128, 4, 128), np.float64)
    for ch in range(4):
        hh = 2 * (ch % 2) + np.arange(128) // 64
        kd[:, ch, :] = np.exp(log_g[hh][None, :] * (63.0 - (np.arange(128) % 64)[:, None])) * sc
    m["kdec_t"] = A(kd.reshape(128, 512))
    m["ident"] = np.eye(128, dtype=f32)
    return m


_NC = {}


def kernel(x, norm_mix, w_in, b_gate, rel_bias, w_ret_out, w_att_out, w_out,
           norm_ffn, w_ffn_gate, w_ffn_up, w_ffn_down, norm_final):
    x = np.asarray(x, dtype=np.float32)
    B, S_, D_ = x.shape
    assert (B, S_, D_) == (NCORES * NSEQ, SEQ, D)
    shared = host_prep(norm_mix, w_in, b_gate, rel_bias, w_ret_out, w_att_out, w_out,
                       norm_ffn, w_ffn_gate, w_ffn_up, w_ffn_down, norm_final)
    if "nc" not in _NC:
        _NC["nc"] = build()
    nc = _NC["nc"]
    xs = x.reshape(NCORES, NSEQ * SEQ, D)
    in_maps = []
    for c in range(NCORES):
        d = dict(shared)
        d["x"] = np.ascontiguousarray(xs[c])
        in_maps.append(d)
    res = run_bass_kernel_spmd(nc, in_maps, core_ids=list(range(NCORES)))
    out = np.stack([np.asarray(r["out"], dtype=np.float32) for r in res.results], axis=0)
    return out.reshape(B, S_, D_)
```

```python
import numpy as np
import concourse.bass as bass
import concourse.mybir as mybir
from concourse.bass_utils import run_bass_kernel_spmd

F32 = mybir.dt.float32
BF16 = mybir.dt.bfloat16
AF = mybir.ActivationFunctionType
ALU = mybir.AluOpType

NCORES = 8
D = 1024
SEQ = 2048
NSEQ = 2
CH = 64
DFF = 2816
NF = DFF // 128
EPS = 1e-6
SAME_ENG_SYNC = True

ENGS = ["pe", "act", "dve", "pool", "sp"]


class Tok:
    __slots__ = ("name", "w", "r", "dsem", "dcnt")

    def __init__(self, name):
        self.name = name
        self.w = None
        self.r = {}
        self.dsem = None
        self.dcnt = 0


class Sched:
    def __init__(self, nc):
        self.nc = nc
        self.q = {e: [] for e in ENGS}
        self.cnt = {e: 0 for e in ENGS}
        self.sem = {e: nc.alloc_semaphore("tl_" + e) for e in ENGS}
        self.seen = {}
        self.nsem = 0

    def _collect(self, eng, reads, writes):
        need = {}

        def add(dep):
            if dep[0] == "eng":
                e, idx = dep[1], dep[2]
                if e == eng and (eng in ("pe", "sp") or not SAME_ENG_SYNC):
                    return
                key = ("eng", e)
                sem = self.sem[e]
                v = idx
            else:
                key, sem, v = dep[1], dep[2], dep[3]
            if v > need.get(key, (None, 0))[1]:
                need[key] = (sem, v)

        for t in reads:
            if t.w is not None:
                add(t.w)
        for t in writes:
            if t.w is not None:
                add(t.w)
            for d in t.r.values():
                add(d)
        waits = []
        for key, (sem, v) in need.items():
            if self.seen.get((eng, key), 0) >= v:
                continue
            self.seen[(eng, key)] = v
            waits.append((sem, v))
        return waits

    def op(self, eng, fn, reads=(), writes=()):
        waits = self._collect(eng, reads, writes)
        self.cnt[eng] += 1
        idx = self.cnt[eng]
        self.q[eng].append(("op", waits, fn))
        for t in writes:
            t.w = ("eng", eng, idx)
            t.r = {}
        for t in reads:
            t.r[("eng", eng)] = ("eng", eng, idx)

    def dma(self, eng, out, in_, tok, reads=(), writes=(), **kw):
        waits = self._collect(eng, reads, writes)
        if tok.dsem is None:
            tok.dsem = self.nc.alloc_semaphore("d_" + tok.name)
            self.nsem += 1
        tok.dcnt += 16
        key = ("dma", tok.name)
        dep = ("dma", key, tok.dsem, tok.dcnt)
        self.q[eng].append(("dma", waits, out, in_, tok.dsem, kw))
        for t in writes:
            t.w = dep
            t.r = {}
        for t in reads:
            t.r[key] = dep

    def wait_tok(self, eng, tok):
        if tok.dsem is not None:
            self.q[eng].append(("wait", [(tok.dsem, tok.dcnt)]))

    def emit(self):
        nc = self.nc
        with nc.Block() as block:
            for eng, attach in (
                ("pe", block.tensor),
                ("act", block.scalar),
                ("dve", block.vector),
                ("pool", block.gpsimd),
                ("sp", block.sync),
            ):
                items = self.q[eng]
                sem = self.sem[eng]

                def body(e, items=items, sem=sem):
                    for it in items:
                        if it[0] == "op":
                            waits = it[1]
                            for s, v in waits[1:]:
                                e.wait_ge(s, v)
                            px = _Proxy(e)
                            ins = it[2](px)
                            if waits:
                                px.first._wait_ge(waits[0][0], waits[0][1])
                            ins.then_inc(sem, 1)
                        elif it[0] == "dma":
                            for s, v in it[1]:
                                e.wait_ge(s, v)
                            e.dma_start(out=it[2], in_=it[3], **it[5]).then_inc(it[4], 16)
                        else:
                            for s, v in it[1]:
                                e.wait_ge(s, v)

                attach(body)
        self.q = {e: [] for e in ENGS}


class _Proxy:
    def __init__(self, e):
        self._e = e
        self.first = None

    def __getattr__(self, name):
        f = getattr(self._e, name)

        def w(*a, **k):
            r = f(*a, **k)
            if self.first is None:
                self.first = r
            return r
        return w


def bc(ap, pos, n):
    a = [list(x) for x in ap.ap]
    a.insert(1 + pos, [0, n])
    return bass.AP(ap.tensor, ap.offset, a)


def dram_bc_rows(t, ncols, nrows=128):
    return bass.AP(t, 0, [[0, nrows], [1, ncols]])


class PsumRing:
    def __init__(self, nc):
        self.t = nc.alloc_psum_tensor("psum_all", [128, 4096], F32)
        self.tok = [Tok("ps%d" % i) for i in range(8)]
        self.p = 0

    NR = 6

    def fixed(self, b, n):
        return self.t[:, b * 512:(b + n) * 512], self.tok[b:b + n]

    def get(self, n=1):
        if n == 3 and self.p % 3 != 0:
            self.p = (self.p + 3 - self.p % 3) % self.NR
        if n == 2 and self.p % 2 == 1:
            self.p = (self.p + 1) % self.NR
        b = self.p
        self.p = (self.p + n) % self.NR
        ap = self.t[:, b * 512:(b + n) * 512]
        return ap, self.tok[b:b + n]


def build(dbg=False, phases=(1, 2, 3), T=NSEQ * SEQ, seq=SEQ, h_in=False, ffn_tiles=None):
    nc = bass.Bass("TRN2", target_bir_lowering=False)
    S = Sched(nc)
    ext_in = lambda name, shape, dt=F32: nc.dram_tensor(name, list(shape), dt, kind="ExternalInput")
    skind = "ExternalOutput" if dbg else "Internal"

    x_d = ext_in("x", [T, D])
    out_d = nc.dram_tensor("out", [T, D], F32, kind="ExternalOutput")
    w1_d = ext_in("w_in_a", [D, 4608])
    wgl_d = ext_in("w_gl", [D, 2048])
    wro_d = ext_in("w_ret_out", [D, D])
    wao_d = ext_in("w_att_out", [512, D])
    wo_d = ext_in("w_out", [D, D])
    wg_d = ext_in("w_gate", [D, DFF])
    wu_d = ext_in("w_up", [D, DFF])
    wd_d = ext_in("w_down", [DFF, D])
    nmix_d = ext_in("norm_mix", [D])
    nffn_d = ext_in("norm_ffn", [D])
    nfin_d = ext_in("norm_final", [D])
    bg_d = ext_in("b_gate_t", [128, 16])
    relb_d = ext_in("relb_t", [128, 8, 640])
    cos_d = ext_in("cos_t", [128, seq])
    sina_d = ext_in("sin_a", [128, seq])
    sinb_d = ext_in("sin_b", [128, seq])
    intra_d = ext_in("intra_t", [128, 512])
    qdec_d = ext_in("qdec_t", [128, 2, 64])
    kdec_d = ext_in("kdec_t", [128, 512])
    cdec_d = ext_in("cdec_t", [128, 2])
    ident_d = ext_in("ident", [128, 128])

    XN = nc.dram_tensor("XN", [8, 128, T], BF16, kind=skind)
    GR = nc.dram_tensor("GR", [8, 128, T], BF16, kind=skind)
    AO = nc.dram_tensor("AO", [4, 128, T], BF16, kind=skind)
    H = nc.dram_tensor("H", [T, D], F32, kind=("ExternalInput" if h_in else skind))

    PS = PsumRing(nc)

    ident = nc.alloc_sbuf_tensor("ident_sb", [128, 128], BF16)
    t_ident = Tok("ident")
    S.dma("pool", ident[:], ident_d.ap(), t_ident, writes=[t_ident])
    eps_t = nc.alloc_sbuf_tensor("eps", [128, 1], F32)
    t_eps = Tok("eps")
    S.op("dve", lambda e: e.memset(eps_t[:], EPS), writes=[t_eps])

    NB = T // 128
    t_XN = [Tok("XN%d" % i) for i in range(NB)]
    t_GR = [Tok("GR%d" % i) for i in range(NB)]
    t_AO = [Tok("AO%d" % i) for i in range(NB)]
    t_H = [Tok("H%d" % i) for i in range(NB)]
    t_out = [Tok("out%d" % i) for i in range(4)]
    ch_XN, ch_GR, ch_AO, ch_H = Tok("chXN"), Tok("chGR"), Tok("chAO"), Tok("chH")

    def finalize(ch, toks):
        for t in toks:
            if t.w is not None:
                t.w = ("dma", ("dma", ch.name), ch.dsem, ch.dcnt)

    def rmsnorm_T(src, t_src, ns, gbc, t_gbc, dstT, t_dst, tmp):
        ss, t_ss = tmp["ss"], tmp["t_ss"]
        for s in range(ns):
            S.op("act", lambda e, s=s: e.activation(out=tmp["junk"][:], in_=src[:, s, :], func=AF.Square,
                                                    accum_out=ss[:, s:s + 1]),
                 reads=[t_src], writes=[tmp["t_junk"], t_ss])
        S.op("act", lambda e: e.activation(out=tmp["rms"][:, 0:ns], in_=ss[:, 0:ns], func=AF.Sqrt,
                                           bias=eps_t[:, 0:1], scale=1.0 / D),
             reads=[t_ss, t_eps], writes=[tmp["t_rms"]])
        S.op("dve", lambda e: e.reciprocal(tmp["rstd"][:, 0:ns], tmp["rms"][:, 0:ns]),
             reads=[tmp["t_rms"]], writes=[tmp["t_rstd"]])
        for s in range(ns):
            xn, t_xn = tmp["xn"][s % 2], tmp["t_xn"][s % 2]
            S.op("dve", lambda e, s=s, xn=xn: e.scalar_tensor_tensor(
                out=xn[:], in0=src[:, s, :], scalar=tmp["rstd"][:, s:s + 1], in1=gbc[:],
                op0=ALU.mult, op1=ALU.mult),
                 reads=[t_src, tmp["t_rstd"], t_gbc], writes=[t_xn])
            pT, t_pT = PS.get(1)
            pTb = pT.bitcast(BF16)

            def tr(e, xn=xn, pTb=pTb):
                for k in range(8):
                    ins = e.transpose(pTb[:, k * 128:(k + 1) * 128], xn[:, k * 128:(k + 1) * 128], ident[:])
                return ins
            S.op("pe", tr, reads=[t_xn, t_ident], writes=t_pT)
            S.op("act", lambda e, s=s, pTb=pTb: e.activation(
                out=dstT[:, :, s * 128:(s + 1) * 128], in_=pTb.rearrange("p (k j) -> p k j", k=8), func=AF.Copy),
                 reads=t_pT, writes=[t_dst])

    def norm_tmp(pfx):
        d = {}
        d["ss"] = nc.alloc_sbuf_tensor(pfx + "ss", [128, 4], F32)
        d["rms"] = nc.alloc_sbuf_tensor(pfx + "rms", [128, 4], F32)
        d["rstd"] = nc.alloc_sbuf_tensor(pfx + "rstd", [128, 4], F32)
        d["junk"] = nc.alloc_sbuf_tensor(pfx + "junk", [128, D], BF16)
        d["xn"] = [nc.alloc_sbuf_tensor(pfx + "xn%d" % i, [128, D], BF16) for i in range(2)]
        for k in ("ss", "rms", "rstd", "junk"):
            d["t_" + k] = Tok(pfx + k)
        d["t_xn"] = [Tok(pfx + "xn%d" % i) for i in range(2)]
        return d


    from contextlib import ExitStack

    def SB(es, name, shape, dt):
        return es.enter_context(nc.sbuf_tensor(name, list(shape), dt))

    def phase_mix(es):
        PS.NR = 6
        PS.p = 0
        TT = 256
        w1 = SB(es, "w1", [128, 8, 4608], BF16)
        t_w1 = [Tok("w1_%d" % i) for i in range(9)]
        w1_v = w1_d.ap().rearrange("(k p) n -> p k n", p=128)
        for i in range(9):
            S.dma("pool", w1[:, :, i * 512:(i + 1) * 512], w1_v[:, :, i * 512:(i + 1) * 512], t_w1[i], writes=[t_w1[i]])
        gmix = SB(es, "gmix", [128, D], F32); t_gmix = Tok("gmix")
        S.dma("sp", gmix[:], dram_bc_rows(nmix_d, D), t_gmix, writes=[t_gmix])
        intra = SB(es, "intra", [128, 512], F32); t_intra = Tok("intra")
        S.dma("sp", intra[:], intra_d.ap(), t_intra, writes=[t_intra])
        kdec = SB(es, "kdec", [128, 512], F32); t_kdec = Tok("kdec")
        S.dma("sp", kdec[:], kdec_d.ap(), t_kdec, writes=[t_kdec])
        qdec = SB(es, "qdec", [128, 2, 64], F32); t_qdec = Tok("qdec")
        S.dma("sp", qdec[:], qdec_d.ap(), t_qdec, writes=[t_qdec])
        cdec = SB(es, "cdec", [128, 2], F32); t_cdec = Tok("cdec")
        S.dma("sp", cdec[:], cdec_d.ap(), t_cdec, writes=[t_cdec])
        expB = SB(es, "expB", [128, 8, 640], BF16); t_expB = Tok("expB")
        relb = SB(es, "relb", [128, 2, 640], F32); t_relb = Tok("relb")
        for hh in range(4):
            S.dma("sp", relb[:], relb_d.ap()[:, hh * 2:hh * 2 + 2, :], t_relb, writes=[t_relb])
            S.op("act", lambda e, hh=hh: e.activation(out=expB[:, hh * 2:hh * 2 + 2, :], in_=relb[:], func=AF.Exp),
                 reads=[t_relb], writes=[t_expB])
        S.op("dve", lambda e: e.memset(expB[64:128, :, 0:64], 0.0), writes=[t_expB])
        S.op("dve", lambda e: e.memset(expB[0:64, :, 576:640], 0.0), writes=[t_expB])

        x_sb = [SB(es, "x_sb%d" % i, [128, 1, D], F32) for i in range(2)]
        t_x = [Tok("x_sb%d" % i) for i in range(2)]
        tmp = norm_tmp_es(es, "a_")
        xnT = SB(es, "xnT", [128, 8, TT], BF16); t_xnT = Tok("xnT")
        cs = [SB(es, "cs%d" % i, [128, 2, TT], F32) for i in range(2)]
        t_cs = [Tok("cs%d" % i) for i in range(2)]
        rt = [SB(es, "rt%d" % i, [128, 2, TT], F32) for i in range(2)]
        t_rt = [Tok("rt%d" % i) for i in range(2)]
        rqT = SB(es, "rqT", [128, 4, TT], BF16); t_rqT = Tok("rqT")
        rq_m = SB(es, "rq_m", [128, 2, 4, TT], BF16); t_rqm = Tok("rq_m")
        rqd = SB(es, "rqd", [128, 4, TT], BF16); t_rqd = Tok("rqd")
        rkT = SB(es, "rkT", [128, 4, TT], BF16); t_rkT = Tok("rkT")
        ktok = [SB(es, "ktok%d" % i, [128, 2, 512], BF16) for i in range(2)]
        t_ktok = [Tok("ktok%d" % i) for i in range(2)]
        rv_sb = SB(es, "rv_sb", [128, 2, D], BF16); t_rv = [Tok("rv0"), Tok("rv1")]
        sg_sb = SB(es, "sg_sb", [128, 2, D], BF16); t_sg = [Tok("sg0"), Tok("sg1")]
        aq_m = SB(es, "aq_m", [128, 2, 4, TT], BF16); t_aqm = Tok("aq_m")
        akr = SB(es, "akr", [128, 4, 768], BF16); t_akr = [Tok("akr%d" % i) for i in range(3)]
        vr = SB(es, "vr", [128, 6, 8, 65], BF16); t_vr = [Tok("vr%d" % i) for i in range(6)]
        e_sb = [SB(es, "e_sb%d" % i, [128, 1280], BF16) for i in range(2)]
        t_e = [Tok("e_sb%d" % i) for i in range(2)]
        p_sb = [SB(es, "p_sb%d" % i, [128, 1280], BF16) for i in range(2)]
        t_p = [Tok("p_sb%d" % i) for i in range(2)]
        sT_sb = SB(es, "sT_sb", [128, 4, 128], BF16); t_sT = Tok("sT_sb")
        St = SB(es, "St", [128, 2, 2, 256], F32); t_S = Tok("St")
        Sbf = [SB(es, "Sbf%d" % i, [128, 2, 2, 2, 256], BF16) for i in range(2)]
        t_Sbf = [Tok("Sbf%d" % i) for i in range(2)]
        stats = SB(es, "stats", [128, 4, 6], F32); t_stats = Tok("stats")
        mv = SB(es, "mv", [128, 4, 2], F32); t_mv = Tok("mv")
        sd = SB(es, "sd", [128, 4], F32); t_sd = Tok("sd")
        rs = SB(es, "rs", [128, 4], F32); t_rs = Tok("rs")
        gnt = SB(es, "gnt", [128, D], F32); t_gnt = [Tok("gnt%d" % i) for i in range(4)]
        gr_sb = SB(es, "gr_sb", [128, D], BF16); t_gr = Tok("gr_sb")
        grT = SB(es, "grT", [128, 8, TT], BF16); t_grT = Tok("grT")
        rden = SB(es, "rden", [128, 8], F32); t_rden = Tok("rden")
        ao_sb = SB(es, "ao_sb", [128, 512], BF16); t_ao = Tok("ao_sb")
        aoT = SB(es, "aoT", [128, 4, TT], BF16); t_aoT = Tok("aoT")

        for buf, tk in ((rq_m, t_rqm), (aq_m, t_aqm), (ktok[0], t_ktok[0]), (ktok[1], t_ktok[1]),
                        (Sbf[0], t_Sbf[0]), (Sbf[1], t_Sbf[1])):
            S.op("pool", lambda e, buf=buf: e.memset(buf[:], 0.0), writes=[tk])
        S.op("pool", lambda e: e.memset(vr[:], 1.0), writes=t_vr)

        NTS = seq // TT
        nseq = T // seq

        def load_x(g):
            b = g % 2
            S.dma("sp", x_sb[b][:, 0, :], x_d.ap()[g * 128:(g + 1) * 128, :], t_x[b], writes=[t_x[b]])

        load_x(0)
        for q_ in range(nseq):
            S.op("dve", lambda e: e.memset(St[:], 0.0), writes=[t_S])
            S.op("pool", lambda e: e.memset(Sbf[0][:], 0.0), writes=[t_Sbf[0]])
            for tt in range(NTS):
                def _body(q_=q_, tt=tt):
                    tg = q_ * NTS + tt
                    tok0 = tg * TT
                    pos0 = tt * TT
                    cb = tg % 2
                    S.dma("sp", cs[cb][:, 0, :], cos_d.ap()[:, pos0:pos0 + TT], t_cs[cb], writes=[t_cs[cb]])
                    S.dma("sp", cs[cb][:, 1, :], sina_d.ap()[:, pos0:pos0 + TT], t_cs[cb], writes=[t_cs[cb]])
                    for s in range(2):
                        g = tg * 2 + s
                        if g + 1 < T // 128:
                            load_x(g + 1)
                        rmsnorm_T(x_sb[g % 2], t_x[g % 2], 1, gmix, t_gmix, xnT[:, :, s * 128:(s + 1) * 128], t_xnT, tmp)
                    S.dma("sp", XN.ap()[:, :, tok0:tok0 + TT].rearrange("k p t -> p k t"), xnT[:], ch_XN,
                          reads=[t_xnT], writes=t_XN[tg * 2:tg * 2 + 2])

                    def fm_proj(col0, nch):
                        ps_, t_ps_ = PS.get(1)

                        def mm(e, ps_=ps_):
                            for c in range(nch):
                                for k in range(8):
                                    ins = e.matmul(ps_[:, c * TT:(c + 1) * TT], w1[:, k, col0 + c * 128:col0 + (c + 1) * 128],
                                                   xnT[:, k, :], start=(k == 0), stop=(k == 7))
                            return ins
                        S.op("pe", mm, reads=[t_xnT, t_w1[col0 // 512]], writes=t_ps_)
                        return ps_.rearrange("p (c t) -> p c t", c=2), t_ps_

                    cosb = bc(cs[cb][:, 0, :], 0, 2)
                    sinb = bc(cs[cb][:, 1, :], 0, 2)
                    for which in range(2):
                        base = which * 512
                        PA, t_PA = fm_proj(base, 2)
                        PB, t_PB = fm_proj(base + 256, 2)
                        dst, t_dst = (rqT, t_rqT) if which == 0 else (rkT, t_rkT)
                        S.op("dve", lambda e, PA=PA: e.tensor_tensor(out=rt[0][:], in0=PA, in1=cosb, op=ALU.mult),
                             reads=t_PA + [t_cs[cb]], writes=[t_rt[0]])
                        S.op("dve", lambda e, PB=PB: e.tensor_tensor(out=rt[1][:], in0=PB, in1=sinb, op=ALU.mult),
                             reads=t_PB + [t_cs[cb]], writes=[t_rt[1]])
                        S.op("pool", lambda e, dst=dst: e.tensor_tensor(out=dst[:, 0:2, :], in0=rt[0][:], in1=rt[1][:], op=ALU.subtract),
                             reads=t_rt, writes=[t_dst])
                        S.op("dve", lambda e, PA=PA: e.tensor_tensor(out=rt[0][:], in0=PA, in1=sinb, op=ALU.mult),
                             reads=t_PA + [t_cs[cb]], writes=[t_rt[0]])
                        S.op("dve", lambda e, PB=PB: e.tensor_tensor(out=rt[1][:], in0=PB, in1=cosb, op=ALU.mult),
                             reads=t_PB + [t_cs[cb]], writes=[t_rt[1]])
                        S.op("pool", lambda e, dst=dst: e.tensor_tensor(out=dst[:, 2:4, :], in0=rt[0][:], in1=rt[1][:], op=ALU.add),
                             reads=t_rt, writes=[t_dst])
                    S.op("act", lambda e: e.activation(out=rq_m[0:64, 0, :, :], in_=rqT[0:64, :, :], func=AF.Copy), reads=[t_rqT], writes=[t_rqm])
                    S.op("act", lambda e: e.activation(out=rq_m[64:128, 1, :, :], in_=rqT[64:128, :, :], func=AF.Copy), reads=[t_rqT], writes=[t_rqm])
                    qdb = bc(qdec[:], 1, TT // 64)
                    for X in range(2):
                        S.op("dve", lambda e, X=X: e.tensor_tensor(
                            out=rqd[:, 2 * X:2 * X + 2, :].rearrange("p a (c n) -> p a c n", n=64),
                            in0=rqT[:, 2 * X:2 * X + 2, :].rearrange("p a (c n) -> p a c n", n=64), in1=qdb, op=ALU.mult),
                             reads=[t_rqT, t_qdec], writes=[t_rqd])
                    for i in range(2):
                        P_, t_P = fm_proj(1024 + i * 256, 2)
                        S.op("act", lambda e, P_=P_, i=i: e.activation(out=aq_m[0:64, 0, 2 * i:2 * i + 2, :], in_=P_[0:64], func=AF.Copy),
                             reads=t_P, writes=[t_aqm])
                        S.op("act", lambda e, P_=P_, i=i: e.activation(out=aq_m[64:128, 1, 2 * i:2 * i + 2, :], in_=P_[64:128], func=AF.Copy),
                             reads=t_P, writes=[t_aqm])
                    slot = (tt % 3)
                    for i in range(2):
                        P_, t_P = fm_proj(1536 + i * 256, 2)
                        S.op("act", lambda e, P_=P_, i=i, slot=slot: e.activation(
                            out=akr[:, 2 * i:2 * i + 2, slot * TT:(slot + 1) * TT], in_=P_, func=AF.Copy),
                             reads=t_P, writes=[t_akr[slot]])

                    for s in range(2):
                        def _body(s=s):
                            G = tt * 2 + s
                            tsl = slice(s * 128, (s + 1) * 128)

                            def tm_proj(col0):
                                ps_, t_ps_ = PS.get(1)

                                def mm(e, ps_=ps_):
                                    for k in range(8):
                                        ins = e.matmul(ps_, xnT[:, k, tsl], w1[:, k, col0:col0 + 512], start=(k == 0), stop=(k == 7))
                                    return ins
                                S.op("pe", mm, reads=[t_xnT, t_w1[col0 // 512]], writes=t_ps_)
                                return ps_, t_ps_
                            for i in range(2):
                                P_, t_P = tm_proj(2048 + i * 512)
                                S.op("act", lambda e, P_=P_, i=i: e.activation(out=rv_sb[:, s, i * 512:(i + 1) * 512], in_=P_, func=AF.Copy),
                                     reads=t_P, writes=[t_rv[s]])
                            for i in range(2):
                                P_, t_P = tm_proj(3072 + i * 512)
                                S.op("act", lambda e, P_=P_, i=i: e.activation(out=sg_sb[:, s, i * 512:(i + 1) * 512], in_=P_, func=AF.Silu),
                                     reads=t_P, writes=[t_sg[s]])
                            P_, t_P = tm_proj(4096)
                            vs = G % 6
                            S.op("act", lambda e, P_=P_, vs=vs: e.activation(out=vr[:, vs, :, 0:64], in_=P_.rearrange("p (h d) -> p h d", h=8),
                                                                             func=AF.Copy), reads=t_P, writes=[t_vr[vs]])
                            kb = G % 2
                            pT, t_pT = PS.get(1)
                            pTb = pT.bitcast(BF16)

                            def trk(e, pTb=pTb):
                                for c in range(4):
                                    ins = e.transpose(pTb[:, c * 128:(c + 1) * 128], rkT[:, c, tsl], ident[:])
                                return ins
                            S.op("pe", trk, reads=[t_rkT, t_ident], writes=t_pT)
                            for ci in range(2):
                                S.op("dve", lambda e, ci=ci, pTb=pTb, kb=kb: e.tensor_tensor(
                                    out=ktok[kb][64 * ci:64 * ci + 64, ci, :], in0=pTb[64 * ci:64 * ci + 64, 0:512],
                                    in1=kdec[64 * ci:64 * ci + 64, :], op=ALU.mult), reads=t_pT + [t_kdec], writes=[t_ktok[kb]])

                            def U_mm(ci):
                                U_, t_U = PS.get(2)
                                Uv = U_.rearrange("p (x r e) -> p x r e", x=2, r=2)

                                def mm(e, Uv=Uv, ci=ci):
                                    for h in range(4):
                                        pr, hs = h // 2, h % 2
                                        for X in range(2):
                                            c0 = (X * 2 + pr) * 128 + hs * 64
                                            ins = e.matmul(Uv[64 * hs:64 * hs + 64, X, pr, :], ktok[kb][:, ci, c0:c0 + 64],
                                                           rv_sb[:, s, h * 256:(h + 1) * 256], start=True, stop=True,
                                                           tile_position=(0, 64 * hs))
                                    return ins
                                S.op("pe", mm, reads=[t_ktok[kb], t_rv[s]], writes=t_U)
                                return Uv, t_U

                            def S_update(Uv, t_U, par):
                                for pr in range(2):
                                    S.op("dve", lambda e, pr=pr, Uv=Uv: e.scalar_tensor_tensor(
                                        out=St[:, :, pr, :], in0=St[:, :, pr, :], scalar=cdec[:, pr:pr + 1], in1=Uv[:, :, pr, :],
                                        op0=ALU.mult, op1=ALU.add), reads=t_U + [t_S, t_cdec], writes=[t_S])
                                S.op("act", lambda e, par=par: e.activation(out=Sbf[par][0:64, 0], in_=St[0:64], func=AF.Copy), reads=[t_S], writes=[t_Sbf[par]])
                                S.op("act", lambda e, par=par: e.activation(out=Sbf[par][64:128, 1], in_=St[64:128], func=AF.Copy), reads=[t_S], writes=[t_Sbf[par]])

                            U0, t_U0 = U_mm(0)
                            sps, t_sps = PS.get(1)

                            def smm(e, sps=sps):
                                for h in range(4):
                                    pr, hs = h // 2, h % 2
                                    for X in range(2):
                                        ins = e.matmul(sps[:, h * 128:(h + 1) * 128], rkT[:, X * 2 + pr, tsl], rq_m[:, hs, X * 2 + pr, tsl],
                                                       start=(X == 0), stop=(X == 1))
                                return ins
                            S.op("pe", smm, reads=[t_rkT, t_rqm], writes=t_sps)
                            S_update(U0, t_U0, 1)
                            S.op("dve", lambda e, sps=sps: e.tensor_tensor(out=sT_sb[:].rearrange("p h n -> p (h n)"), in0=sps, in1=intra[:], op=ALU.mult),
                                 reads=t_sps + [t_intra], writes=[t_sT])
                            U1, t_U1 = U_mm(1)
                            ops_, t_ops = PS.get(2)

                            def omm(e, ops_=ops_):
                                for h in range(4):
                                    pr, hs = h // 2, h % 2
                                    e.matmul(ops_[:, h * 256:(h + 1) * 256], sT_sb[:, h, :], rv_sb[:, s, h * 256:(h + 1) * 256], start=True, stop=False)
                                    for ci in range(2):
                                        for X in range(2):
                                            c0 = s * 128 + ci * 64
                                            ins = e.matmul(ops_[64 * ci:64 * ci + 64, h * 256:(h + 1) * 256], rqd[:, X * 2 + pr, c0:c0 + 64],
                                                           Sbf[ci][:, hs, X, pr, :], start=False, stop=(ci == 1 and X == 1),
                                                           tile_position=(0, 64 * ci))
                                return ins
                            S.op("pe", omm, reads=[t_sT, t_rv[s], t_rqd, t_Sbf[0], t_Sbf[1]], writes=t_ops)
                            S_update(U1, t_U1, 0)
                            for h in range(4):
                                S.op("dve", lambda e, h=h, ops_=ops_: e.bn_stats(out=stats[:, h, :], in_=ops_[:, h * 256:(h + 1) * 256]),
                                     reads=t_ops, writes=[t_stats])
                            for h in range(4):
                                S.op("dve", lambda e, h=h: e.bn_aggr(out=mv[:, h, :], in_=stats[:, h, :]), reads=[t_stats], writes=[t_mv])
                            S.op("act", lambda e: e.activation(out=sd[:], in_=mv[:, :, 1], func=AF.Sqrt, bias=eps_t[:, 0:1], scale=1.0),
                                 reads=[t_mv, t_eps], writes=[t_sd])
                            S.op("dve", lambda e: e.reciprocal(rs[:], sd[:]), reads=[t_sd], writes=[t_rs])
                            for h in range(4):
                                hsl = slice(h * 256, (h + 1) * 256)
                                S.op("dve", lambda e, h=h, hsl=hsl, ops_=ops_: e.scalar_tensor_tensor(
                                    out=gnt[:, hsl], in0=ops_[:, hsl], scalar=mv[:, h, 0:1], in1=sg_sb[:, s, hsl],
                                    op0=ALU.subtract, op1=ALU.mult), reads=t_ops + [t_mv, t_sg[s]], writes=[t_gnt[h]])
                                S.op("dve", lambda e, h=h, hsl=hsl: e.tensor_scalar(out=gr_sb[:, hsl], in0=gnt[:, hsl], scalar1=rs[:, h:h + 1], scalar2=None, op0=ALU.mult),
                                     reads=[t_gnt[h], t_rs], writes=[t_gr])
                            pT, t_pT = PS.get(1)
                            pTb = pT.bitcast(BF16)

                            def trg(e, pTb=pTb):
                                for k in range(8):
                                    ins = e.transpose(pTb[:, k * 128:(k + 1) * 128], gr_sb[:, k * 128:(k + 1) * 128], ident[:])
                                return ins
                            S.op("pe", trg, reads=[t_gr, t_ident], writes=t_pT)
                            S.op("act", lambda e, pTb=pTb: e.activation(out=grT[:, :, tsl], in_=pTb.rearrange("p (k j) -> p k j", k=8), func=AF.Copy),
                                 reads=t_pT, writes=[t_grT])

                            nb = min(4, G) + 1
                            ncol = nb * 128
                            oa, t_oa = PS.fixed(6, 2)
                            for hc in range(4):
                                s3, t_s3 = PS.get(3)

                                def amm(e, s3=s3, hc=hc):
                                    for dl in range(nb):
                                        J = G - dl
                                        rc = ((J // 2) % 3) * TT + (J % 2) * 128
                                        ins = e.matmul(s3[:, dl * 256:(dl + 1) * 256], akr[:, hc, rc:rc + 128], aq_m[:, :, hc, tsl],
                                                       start=True, stop=True)
                                    return ins
                                S.op("pe", amm, reads=t_akr + [t_aqm], writes=t_s3)
                                eb = hc % 2
                                S.op("act", lambda e, s3=s3, eb=eb: e.activation(out=e_sb[eb][:, 0:2 * ncol], in_=s3[:, 0:2 * ncol], func=AF.Exp, scale=0.125),
                                     reads=t_s3, writes=[t_e[eb]])
                                S.op("dve", lambda e, eb=eb, hc=hc: e.tensor_tensor(
                                    out=p_sb[eb][:, 0:2 * ncol].rearrange("p (d h n) -> p d h n", h=2, n=128),
                                    in0=e_sb[eb][:, 0:2 * ncol].rearrange("p (d h n) -> p d h n", h=2, n=128),
                                    in1=expB[:, 2 * hc:2 * hc + 2, 0:ncol].rearrange("p h (d n) -> p d h n", n=128), op=ALU.mult),
                                     reads=[t_e[eb], t_expB], writes=[t_p[eb]])

                                def avm(e, eb=eb, hc=hc, oa=oa):
                                    for hs in range(2):
                                        h = 2 * hc + hs
                                        ob = (h // 4) * 512 + (h % 4) * 65
                                        for dl in range(nb):
                                            J = G - dl
                                            ins = e.matmul(oa[:, ob:ob + 65], p_sb[eb][:, dl * 256 + hs * 128:dl * 256 + hs * 128 + 128],
                                                           vr[:, J % 6, h, :], start=(dl == 0), stop=(dl == nb - 1))
                                    return ins
                                S.op("pe", avm, reads=[t_p[eb]] + t_vr, writes=t_oa)
                            for b2 in range(2):
                                oav = oa[:, b2 * 512:b2 * 512 + 260].rearrange("p (h d) -> p h d", d=65)
                                S.op("dve", lambda e, oav=oav, b2=b2: e.reciprocal(rden[:, b2 * 4:b2 * 4 + 4], oav[:, :, 64]),
                                     reads=t_oa, writes=[t_rden])
                                S.op("dve", lambda e, oav=oav, b2=b2: e.tensor_tensor(
                                    out=ao_sb[:, b2 * 256:(b2 + 1) * 256].rearrange("p (h d) -> p h d", d=64), in0=oav[:, :, 0:64],
                                    in1=bc(rden[:, b2 * 4:b2 * 4 + 4], 1, 64), op=ALU.mult), reads=t_oa + [t_rden], writes=[t_ao])
                            pT, t_pT = PS.get(1)
                            pTb = pT.bitcast(BF16)

                            def tra(e, pTb=pTb):
                                for k in range(4):
                                    ins = e.transpose(pTb[:, k * 128:(k + 1) * 128], ao_sb[:, k * 128:(k + 1) * 128], ident[:])
                                return ins
                            S.op("pe", tra, reads=[t_ao, t_ident], writes=t_pT)
                            S.op("act", lambda e, pTb=pTb: e.activation(out=aoT[:, :, tsl], in_=pTb[:, 0:512].rearrange("p (k j) -> p k j", k=4), func=AF.Copy),
                                 reads=t_pT, writes=[t_aoT])
                        _body()
                    S.dma("sp", GR.ap()[:, :, tok0:tok0 + TT].rearrange("k p t -> p k t"), grT[:], ch_GR,
                          reads=[t_grT], writes=t_GR[tg * 2:tg * 2 + 2])
                    S.dma("sp", AO.ap()[:, :, tok0:tok0 + TT].rearrange("k p t -> p k t"), aoT[:], ch_AO,
                          reads=[t_aoT], writes=t_AO[tg * 2:tg * 2 + 2])
                _body()

    def phase_merge(es):
        PS.NR = 8
        PS.p = 0
        TT = 512
        NT = T // TT
        wgl = SB(es, "wgl", [128, 8, 2048], BF16)
        wro = SB(es, "wro", [128, 8, D], BF16)
        wao = SB(es, "wao", [128, 4, D], BF16)
        wo = SB(es, "wo", [128, 8, D], BF16)
        t_wgl = [Tok("wgl%d" % i) for i in range(4)]
        t_wro = [Tok("wro%d" % i) for i in range(2)]
        t_wao = [Tok("wao%d" % i) for i in range(2)]
        t_wo = [Tok("wo%d" % i) for i in range(2)]
        v = wgl_d.ap().rearrange("(k p) n -> p k n", p=128)
        for i in range(4):
            S.dma("pool", wgl[:, :, i * 512:(i + 1) * 512], v[:, :, i * 512:(i + 1) * 512], t_wgl[i], writes=[t_wgl[i]])
        for wsb, wdr, tk in ((wro, wro_d, t_wro), (wao, wao_d, t_wao), (wo, wo_d, t_wo)):
            v = wdr.ap().rearrange("(k p) n -> p k n", p=128)
            for i in range(2):
                S.dma("pool", wsb[:, :, i * 512:(i + 1) * 512], v[:, :, i * 512:(i + 1) * 512], tk[i], writes=[tk[i]])
        bg = SB(es, "bg", [128, 16], F32); t_bg = Tok("bg")
        S.dma("sp", bg[:], bg_d.ap(), t_bg, writes=[t_bg])
        x_sb = [SB(es, "xb%d" % i, [128, 4, D], F32) for i in range(2)]
        t_x = [Tok("xb%d" % i) for i in range(2)]
        xnT = [SB(es, "xnTb%d" % i, [128, 8, TT], BF16) for i in range(2)]
        t_xnT = [Tok("xnTb%d" % i) for i in range(2)]
        grT = [SB(es, "grTb%d" % i, [128, 8, TT], BF16) for i in range(2)]
        t_grT = [Tok("grTb%d" % i) for i in range(2)]
        aoT = [SB(es, "aoTb%d" % i, [128, 4, TT], BF16) for i in range(2)]
        t_aoT = [Tok("aoTb%d" % i) for i in range(2)]
        mT = SB(es, "mT", [128, 8, TT], BF16); t_mT = [Tok("mT%d" % i) for i in range(8)]
        gs = [SB(es, "gs%d" % i, [128, TT], F32) for i in range(4)]
        t_gs = [Tok("gs%d" % i) for i in range(4)]
        tt_ = [SB(es, "tt%d" % i, [128, TT], F32) for i in range(4)]
        t_tt = [Tok("tt%d" % i) for i in range(4)]

        def load(t):
            b = t % 2
            sl = slice(t * TT, (t + 1) * TT)
            S.dma("sp", x_sb[b][:], x_d.ap()[sl, :].rearrange("(s p) d -> p s d", p=128), t_x[b], writes=[t_x[b]])
            S.dma("sp", xnT[b][:], XN.ap()[:, :, sl].rearrange("k p t -> p k t"), t_xnT[b], reads=t_XN[t * 4:t * 4 + 4], writes=[t_xnT[b]])
            S.dma("sp", grT[b][:], GR.ap()[:, :, sl].rearrange("k p t -> p k t"), t_grT[b], reads=t_GR[t * 4:t * 4 + 4], writes=[t_grT[b]])
            S.dma("sp", aoT[b][:], AO.ap()[:, :, sl].rearrange("k p t -> p k t"), t_aoT[b], reads=t_AO[t * 4:t * 4 + 4], writes=[t_aoT[b]])

        load(0)
        for t in range(NT):
            def _body(t=t):
                b = t % 2
                if t + 1 < NT:
                    load(t + 1)
                for c in range(8):
                    csl = slice(c * 128, (c + 1) * 128)
                    yr, t_yr = PS.get(1)
                    ya, t_ya = PS.get(1)
                    glr, t_glr = PS.get(1)
                    gla, t_gla = PS.get(1)

                    def mm(e, yr=yr, ya=ya, glr=glr, gla=gla, csl=csl, c=c):
                        for k in range(8):
                            e.matmul(glr, wgl[:, k, csl], xnT[b][:, k, :], start=(k == 0), stop=(k == 7))
                        for k in range(8):
                            e.matmul(gla, wgl[:, k, 1024 + c * 128:1024 + (c + 1) * 128], xnT[b][:, k, :], start=(k == 0), stop=(k == 7))
                        for k in range(8):
                            e.matmul(yr, wro[:, k, csl], grT[b][:, k, :], start=(k == 0), stop=(k == 7))
                        for k in range(4):
                            ins = e.matmul(ya, wao[:, k, csl], aoT[b][:, k, :], start=(k == 0), stop=(k == 3))
                        return ins
                    S.op("pe", mm, reads=[t_xnT[b], t_grT[b], t_aoT[b]] + t_wgl + t_wro + t_wao, writes=t_yr + t_ya + t_glr + t_gla)
                    i0 = (c % 2) * 2
                    S.op("act", lambda e, glr=glr, i0=i0, c=c: e.activation(out=gs[i0][:], in_=glr, func=AF.Sigmoid, bias=bg[:, c:c + 1], scale=1.0),
                         reads=t_glr + [t_bg], writes=[t_gs[i0]])
                    S.op("act", lambda e, gla=gla, i0=i0, c=c: e.activation(out=gs[i0 + 1][:], in_=gla, func=AF.Sigmoid, bias=bg[:, 8 + c:9 + c], scale=1.0),
                         reads=t_gla + [t_bg], writes=[t_gs[i0 + 1]])
                    S.op("dve", lambda e, yr=yr, i0=i0: e.tensor_tensor(out=tt_[i0][:], in0=yr, in1=gs[i0][:], op=ALU.mult),
                         reads=t_yr + [t_gs[i0]], writes=[t_tt[i0]])
                    S.op("dve", lambda e, ya=ya, i0=i0: e.tensor_tensor(out=tt_[i0 + 1][:], in0=ya, in1=gs[i0 + 1][:], op=ALU.mult),
                         reads=t_ya + [t_gs[i0 + 1]], writes=[t_tt[i0 + 1]])
                    S.op("pool", lambda e, i0=i0, c=c: e.tensor_tensor(out=mT[:, c, :], in0=tt_[i0][:], in1=tt_[i0 + 1][:], op=ALU.add),
                         reads=[t_tt[i0], t_tt[i0 + 1]], writes=[t_mT[c]])
                for s in range(4):
                    for hf in range(2):
                        ops_, t_ops = PS.get(1)

                        def mm2(e, s=s, hf=hf, ops_=ops_):
                            for k in range(8):
                                ins = e.matmul(ops_, mT[:, k, s * 128:(s + 1) * 128], wo[:, k, hf * 512:(hf + 1) * 512], start=(k == 0), stop=(k == 7))
                            return ins
                        S.op("pe", mm2, reads=t_mT + t_wo, writes=t_ops)
                        S.op("dve", lambda e, s=s, hf=hf, ops_=ops_, b=b: e.tensor_tensor(
                            out=x_sb[b][:, s, hf * 512:(hf + 1) * 512], in0=ops_, in1=x_sb[b][:, s, hf * 512:(hf + 1) * 512], op=ALU.add),
                             reads=t_ops + [t_x[b]], writes=[t_x[b]])
                S.dma("sp", H.ap()[t * TT:(t + 1) * TT, :].rearrange("(s p) d -> p s d", p=128), x_sb[b][:], ch_H,
                      reads=[t_x[b]], writes=t_H[t * 4:t * 4 + 4])
            _body()


    def phase_ffn(es):
        PS.NR = 8
        PS.p = 0
        wg = SB(es, "wg", [128, 8, DFF], BF16)
        wu = SB(es, "wu", [128, 8, DFF], BF16)
        wd = SB(es, "wd", [128, NF, D], BF16)
        grp = [(0, 6), (6, 12), (12, 17), (17, 22)]
        t_wg = [Tok("wg%d" % i) for i in range(4)]
        t_wu = [Tok("wu%d" % i) for i in range(4)]
        t_wd = [Tok("wd%d" % i) for i in range(4)]
        f2g = {}
        for gi, (a, b) in enumerate(grp):
            for f in range(a, b):
                f2g[f] = gi
        gffn = SB(es, "gffn", [128, D], F32)
        gfin = SB(es, "gfin", [128, D], F32)
        t_gffn, t_gfin = Tok("gffn"), Tok("gfin")
        S.dma("sp", gffn[:], dram_bc_rows(nffn_d, D), t_gffn, writes=[t_gffn])
        S.dma("sp", gfin[:], dram_bc_rows(nfin_d, D), t_gfin, writes=[t_gfin])
        wg_v = wg_d.ap().rearrange("(k p) n -> p k n", p=128)
        wu_v = wu_d.ap().rearrange("(k p) n -> p k n", p=128)
        wd_v = wd_d.ap().rearrange("(f p) n -> p f n", p=128)
        for gi, (a, b) in enumerate(grp):
            S.dma("pool", wg[:, :, a * 128:b * 128], wg_v[:, :, a * 128:b * 128], t_wg[gi], writes=[t_wg[gi]])
            S.dma("pool", wu[:, :, a * 128:b * 128], wu_v[:, :, a * 128:b * 128], t_wu[gi], writes=[t_wu[gi]])
        for gi, (a, b) in enumerate(grp):
            S.dma("pool", wd[:, a:b, :], wd_v[:, a:b, :], t_wd[gi], writes=[t_wd[gi]])

        TT = 512
        NT = T // TT
        if ffn_tiles is not None:
            NT = ffn_tiles
        NHB = 4
        h_sb = [SB(es, "h_sb%d" % i, [128, 1, D], F32) for i in range(NHB)]
        t_h = [Tok("h_sb%d" % i) for i in range(NHB)]
        hnT = SB(es, "hnT", [128, 8, TT], BF16)
        t_hnT = Tok("hnT")
        actT = SB(es, "actT", [128, NF, TT], BF16)
        t_actT = [Tok("actT%d" % f) for f in range(NF)]
        sg = [SB(es, "sg%d" % i, [128, TT], F32) for i in range(2)]
        t_sg = [Tok("sg%d" % i) for i in range(2)]
        tmp = norm_tmp_es(es, "f_")
        ctr = [0]

        def load_h(g):
            b = ctr[0] % NHB
            ctr[0] += 1
            S.dma("sp", h_sb[b][:, 0, :], H.ap()[g * 128:(g + 1) * 128, :], t_h[b], reads=[t_H[g]], writes=[t_h[b]])
            return b

        for t in range(NT):
            def _body(t=t):
                bufs = [load_h(t * 4 + s) for s in range(4)]
                for s in range(4):
                    rmsnorm_T(h_sb[bufs[s]], t_h[bufs[s]], 1, gffn, t_gffn, hnT[:, :, s * 128:(s + 1) * 128], t_hnT, tmp)
                bufs2 = [load_h(t * 4 + s) for s in range(4)]
                for f in range(NF):
                    gps, t_gps = PS.get(1)
                    ups, t_ups = PS.get(1)
                    gi = f2g[f]

                    def mm(e, f=f, gps=gps, ups=ups):
                        for k in range(8):
                            e.matmul(gps, wg[:, k, f * 128:(f + 1) * 128], hnT[:, k, :], start=(k == 0), stop=(k == 7))
                        for k in range(8):
                            ins = e.matmul(ups, wu[:, k, f * 128:(f + 1) * 128], hnT[:, k, :], start=(k == 0), stop=(k == 7))
                        return ins
                    S.op("pe", mm, reads=[t_hnT, t_wg[gi], t_wu[gi]], writes=t_gps + t_ups)
                    sgb, t_sgb = sg[f % 2], t_sg[f % 2]
                    S.op("act", lambda e, gps=gps, sgb=sgb: e.activation(out=sgb[:], in_=gps, func=AF.Silu),
                         reads=t_gps, writes=[t_sgb])
                    S.op("dve", lambda e, f=f, ups=ups, sgb=sgb: e.tensor_tensor(
                        out=actT[:, f, :], in0=ups, in1=sgb[:], op=ALU.mult),
                         reads=t_ups + [t_sgb], writes=[t_actT[f]])
                ss, rms, rstd = tmp["ss"], tmp["rms"], tmp["rstd"]
                for s in range(4):
                    hb, t_hb = h_sb[bufs2[s]], t_h[bufs2[s]]
                    for hf in range(2):
                        ops_, t_ops = PS.get(1)

                        def mm2(e, s=s, hf=hf, ops_=ops_):
                            for f in range(NF):
                                ins = e.matmul(ops_, actT[:, f, s * 128:(s + 1) * 128], wd[:, f, hf * 512:(hf + 1) * 512],
                                               start=(f == 0), stop=(f == NF - 1))
                            return ins
                        S.op("pe", mm2, reads=t_actT + t_wd, writes=t_ops)
                        S.op("dve", lambda e, hf=hf, ops_=ops_, hb=hb: e.tensor_tensor(
                            out=hb[:, 0, hf * 512:(hf + 1) * 512], in0=ops_, in1=hb[:, 0, hf * 512:(hf + 1) * 512],
                            op=ALU.add), reads=t_ops + [t_hb], writes=[t_hb])
                    S.op("act", lambda e, hb=hb: e.activation(out=tmp["junk"][:], in_=hb[:, 0, :], func=AF.Square,
                                                            accum_out=ss[:, 0:1]),
                         reads=[t_hb], writes=[tmp["t_junk"], tmp["t_ss"]])
                    S.op("act", lambda e: e.activation(out=rms[:, 0:1], in_=ss[:, 0:1], func=AF.Sqrt, bias=eps_t[:, 0:1],
                                                       scale=1.0 / D), reads=[tmp["t_ss"], t_eps], writes=[tmp["t_rms"]])
                    S.op("dve", lambda e: e.reciprocal(rstd[:, 0:1], rms[:, 0:1]), reads=[tmp["t_rms"]], writes=[tmp["t_rstd"]])
                    S.op("dve", lambda e, hb=hb: e.scalar_tensor_tensor(
                        out=hb[:, 0, :], in0=hb[:, 0, :], scalar=rstd[:, 0:1], in1=gfin[:],
                        op0=ALU.mult, op1=ALU.mult), reads=[t_hb, tmp["t_rstd"], t_gfin], writes=[t_hb])
                    g = t * 4 + s
                    S.dma("sp", out_d.ap()[g * 128:(g + 1) * 128, :], hb[:, 0, :], t_out[g % 4], reads=[t_hb], writes=[t_out[g % 4]])
            _body()
        for tk in t_out:
            S.wait_tok("sp", tk)

    def norm_tmp_es(es, pfx):
        d = {}
        d["ss"] = SB(es, pfx + "ss", [128, 4], F32)
        d["rms"] = SB(es, pfx + "rms", [128, 4], F32)
        d["rstd"] = SB(es, pfx + "rstd", [128, 4], F32)
        d["junk"] = SB(es, pfx + "junk", [128, D], BF16)
        d["xn"] = [SB(es, pfx + "xn%d" % i, [128, D], BF16) for i in range(2)]
        for k in ("ss", "rms", "rstd", "junk"):
            d["t_" + k] = Tok(pfx + k)
        d["t_xn"] = [Tok(pfx + "xn%d" % i) for i in range(2)]
        return d

    if 1 in phases:
        with ExitStack() as es:
            phase_mix(es)
            finalize(ch_XN, t_XN)
            finalize(ch_GR, t_GR)
            finalize(ch_AO, t_AO)
            S.emit()
    if 2 in phases:
        with ExitStack() as es:
            phase_merge(es)
            finalize(ch_H, t_H)
            S.emit()
    if 3 in phases:
        with ExitStack() as es:
            phase_ffn(es)
            S.emit()
    return nc


def host_prep(norm_mix, w_in, b_gate, rel_bias, w_ret_out, w_att_out, w_out,
              norm_ffn, w_ffn_gate, w_ffn_up, w_ffn_down, norm_final, seq=SEQ):
    f32 = np.float32
    A = lambda a: np.ascontiguousarray(np.asarray(a, dtype=f32))
    w_in = np.asarray(w_in, dtype=f32)[0]
    j = np.arange(128)
    perm_rq = np.concatenate([(2 * (ch % 2) + j // 64) * 128 + (ch // 2) * 64 + j % 64 for ch in range(4)])
    cols_a = np.concatenate([perm_rq, 512 + perm_rq, 3072 + np.arange(512), 3584 + np.arange(512),
                             1024 + np.arange(1024), 2048 + np.arange(1024), 4096 + np.arange(512)])
    m = {}
    m["w_in_a"] = A(w_in[:, cols_a])
    m["w_gl"] = A(w_in[:, 4608:6656])
    m["w_ret_out"] = A(np.asarray(w_ret_out)[0])
    m["w_att_out"] = A(np.asarray(w_att_out)[0])
    m["w_out"] = A(np.asarray(w_out)[0])
    m["w_gate"] = A(np.asarray(w_ffn_gate)[0])
    m["w_up"] = A(np.asarray(w_ffn_up)[0])
    m["w_down"] = A(np.asarray(w_ffn_down)[0])
    m["norm_mix"] = A(np.asarray(norm_mix)[0])
    m["norm_ffn"] = A(np.asarray(norm_ffn)[0])
    m["norm_final"] = A(np.asarray(norm_final))
    m["b_gate_t"] = A(np.asarray(b_gate, dtype=f32)[0].reshape(16, 128).T)
    rb = np.asarray(rel_bias, dtype=f32)[0]
    jj = np.arange(128)[:, None]
    col = np.arange(640)[None, :]
    idx = np.clip(col - jj, -63, 256) + 63
    m["relb_t"] = A(np.transpose(rb[:, idx], (1, 0, 2)))
    p = np.arange(128)
    freqs = (10000.0 ** (-np.arange(0, 128, 2, dtype=f32) / f32(128))).astype(f32)
    pos = np.arange(seq, dtype=f32)
    ang = (pos[None, :] * freqs[p % 64][:, None]).astype(f32)
    m["cos_t"] = A(np.cos(ang))
    m["sin_a"] = A(np.sin(ang))
    m["sin_b"] = A(np.sin(ang))
    log_g = np.log(1.0 - 2.0 ** (-5.0 - np.arange(4, dtype=np.float64)))
    sc = 128.0 ** -0.5
    mm_ = np.arange(128)[:, None]
    nn = np.arange(128)[None, :]
    blk = (mm_ // 64 == nn // 64)
    intra = np.zeros((128, 4, 128), np.float64)
    for h in range(4):
        intra[:, h, :] = np.where(blk, np.exp(log_g[h] * np.abs(nn % 64 - mm_ % 64)) * sc, 0.0)
    m["intra_t"] = A(intra.reshape(128, 512))
    qd = np.zeros((128, 2, 64), np.float64)
    cd = np.zeros((128, 2), np.float64)
    for pr in range(2):
        hh = 2 * pr + p // 64
        qd[:, pr, :] = np.exp(log_g[hh][:, None] * (np.arange(64)[None, :] + 1.0))
        cd[:, pr] = np.exp(log_g[hh] * 64.0)
    m["qdec_t"] = A(qd)
    m["cdec_t"] = A(cd)
    kd = np.zeros((128, 4, 128), np.float64)
    for ch in range(4):
        hh = 2 * (ch % 2) + np.arange(128) // 64
        kd[:, ch, :] = np.exp(log_g[hh][None, :] * (63.0 - (np.arange(128) % 64)[:, None])) * sc
    m["kdec_t"] = A(kd.reshape(128, 512))
    m["ident"] = np.eye(128, dtype=f32)
    return m


_NC = {}


def kernel(x, norm_mix, w_in, b_gate, rel_bias, w_ret_out, w_att_out, w_out,
           norm_ffn, w_ffn_gate, w_ffn_up, w_ffn_down, norm_final):
    x = np.asarray(x, dtype=np.float32)
    B, S_, D_ = x.shape
    assert (B, S_, D_) == (NCORES * NSEQ, SEQ, D)
    shared = host_prep(norm_mix, w_in, b_gate, rel_bias, w_ret_out, w_att_out, w_out,
                       norm_ffn, w_ffn_gate, w_ffn_up, w_ffn_down, norm_final)
    if "nc" not in _NC:
        _NC["nc"] = build()
    nc = _NC["nc"]
    xs = x.reshape(NCORES, NSEQ * SEQ, D)
    in_maps = []
    for c in range(NCORES):
        d = dict(shared)
        d["x"] = np.ascontiguousarray(xs[c])
        in_maps.append(d)
    res = run_bass_kernel_spmd(nc, in_maps, core_ids=list(range(NCORES)))
    out = np.stack([np.asarray(r["out"], dtype=np.float32) for r in res.results], axis=0)
    return out.reshape(B, S_, D_)
```
